# Optimizing a Trainium2 kernel written in Bass

```python
import math
import jax, jax.numpy as jnp
from jax import lax
import numpy as np

D_MODEL = 1024
BATCH = 16
SEQ = 2048
DEPTH = 1

CHUNK = 64
Q_BLOCK = 128
HEAD_DIM = 64
DIFF_HEADS = 4
DIFF_V_DIM = 2 * HEAD_DIM
DIFF_WIDTH = DIFF_HEADS * DIFF_V_DIM
FOX_HEADS = 8
FOX_WIDTH = FOX_HEADS * HEAD_DIM
N_BRANCHES = 2
D_FF = 4 * D_MODEL
REL_BUCKETS = 32
REL_MAX_DIST = 128
EPS = 1e-6
FORGET_BIAS_INIT = 2.0

DIFF_QK_COLS = DIFF_HEADS * 2 * HEAD_DIM
COL_SIZES = [DIFF_QK_COLS, DIFF_QK_COLS, DIFF_WIDTH,
             FOX_WIDTH, FOX_WIDTH, FOX_WIDTH, FOX_HEADS,
             N_BRANCHES * D_MODEL]
IN_COLS = sum(COL_SIZES)
SPLITS = [sum(COL_SIZES[:i + 1]) for i in range(len(COL_SIZES) - 1)]

kernel_name = "hybrid_diff_fox_gated_encoder"


def rmsnorm(x, g):
    xf = x.astype(jnp.float32)
    y = xf * lax.rsqrt(jnp.mean(xf * xf, axis=-1, keepdims=True) + EPS)
    return (y * g.astype(jnp.float32)).astype(x.dtype)


def lambda_init_fn(layer_idx):
    return 0.8 - 0.6 * math.exp(-0.3 * layer_idx)


def rel_bucket(rel):
    nb = REL_BUCKETS // 2
    ret = jnp.where(rel > 0, nb, 0)
    n = jnp.abs(rel)
    max_exact = nb // 2
    nf = jnp.maximum(n, 1).astype(jnp.float32)
    large = max_exact + (jnp.log(nf / max_exact) / math.log(REL_MAX_DIST / max_exact)
                         * (nb - max_exact)).astype(jnp.int32)
    large = jnp.minimum(large, nb - 1)
    return ret + jnp.where(n < max_exact, n, large)


def attention_branches(q_d, k_d, v_d, q_f, k_f, v_f, fcum, rel_table, lam):
    S = q_d.shape[1]
    scale = HEAD_DIM ** -0.5
    pos = jnp.arange(S, dtype=jnp.int32)
    outs_d, outs_f = [], []
    for blk in range(S // Q_BLOCK):
        q0 = blk * Q_BLOCK
        kend = q0 + Q_BLOCK
        qp = pos[q0:kend]
        kp = pos[:kend]
        rel = kp[None, :] - qp[:, None]
        bias = jnp.transpose(rel_table[rel_bucket(rel)], (2, 0, 1))
        chunk_mask = (kp[None, :] // CHUNK) <= (qp[:, None] // CHUNK)
        frame_mask = kp[None, :] <= qp[:, None]

        s_d = jnp.einsum('bqhmd,bkhmd->bhmqk', q_d[:, q0:kend], k_d[:, :kend]).astype(jnp.float32)
        s_d = s_d * scale + bias[None, :, None].astype(jnp.float32)
        s_d = jnp.where(chunk_mask, s_d, -jnp.inf)
        p_d = jax.nn.softmax(s_d, axis=-1)
        w_d = p_d[:, :, 0] - lam * p_d[:, :, 1]
        outs_d.append(jnp.einsum('bhqk,bkhe->bqhe', w_d.astype(v_d.dtype), v_d[:, :kend]))

        s_f = jnp.einsum('bqhd,bkhd->bhqk', q_f[:, q0:kend], k_f[:, :kend]).astype(jnp.float32)
        s_f = s_f * scale + (fcum[:, :, q0:kend, None] - fcum[:, :, None, :kend])
        s_f = jnp.where(frame_mask, s_f, -jnp.inf)
        p_f = jax.nn.softmax(s_f, axis=-1)
        outs_f.append(jnp.einsum('bhqk,bkhd->bqhd', p_f.astype(v_f.dtype), v_f[:, :kend]))
    return jnp.concatenate(outs_d, axis=1), jnp.concatenate(outs_f, axis=1)


def hybrid_layer(x, layer_idx, g_mix, w_in, b_f, lam_q1, lam_k1, lam_q2, lam_k2, g_subln,
                 w_pa, w_pb, w_o, g_mlp, w_1, w_2, rel_table):
    B, S, _ = x.shape
    h = rmsnorm(x, g_mix)
    proj = h @ w_in
    dq, dk, dv, fq, fk, fv, fl, gl = jnp.split(proj, SPLITS, axis=-1)
    q_d = dq.reshape(B, S, DIFF_HEADS, 2, HEAD_DIM)
    k_d = dk.reshape(B, S, DIFF_HEADS, 2, HEAD_DIM)
    v_d = dv.reshape(B, S, DIFF_HEADS, DIFF_V_DIM)
    q_f = fq.reshape(B, S, FOX_HEADS, HEAD_DIM)
    k_f = fk.reshape(B, S, FOX_HEADS, HEAD_DIM)
    v_f = fv.reshape(B, S, FOX_HEADS, HEAD_DIM)

    logf = jax.nn.log_sigmoid((fl + b_f).astype(jnp.float32))
    fcum = jnp.transpose(jnp.cumsum(logf, axis=1), (0, 2, 1))

    lam_init = lambda_init_fn(layer_idx)
    lam = (jnp.exp(jnp.sum(lam_q1.astype(jnp.float32) * lam_k1.astype(jnp.float32)))
           - jnp.exp(jnp.sum(lam_q2.astype(jnp.float32) * lam_k2.astype(jnp.float32)))
           + lam_init)

    o_d, o_f = attention_branches(q_d, k_d, v_d, q_f, k_f, v_f, fcum, rel_table, lam)
    o_d = (rmsnorm(o_d, g_subln) * (1.0 - lam_init)).reshape(B, S, DIFF_WIDTH)
    o_f = o_f.reshape(B, S, FOX_WIDTH)

    gates = jax.nn.sigmoid(gl.astype(jnp.float32)).astype(x.dtype).reshape(B, S, N_BRANCHES, D_MODEL)
    merged = gates[:, :, 0] * (o_d @ w_pa) + gates[:, :, 1] * (o_f @ w_pb)
    x = x + merged @ w_o

    h2 = rmsnorm(x, g_mlp)
    x = x + jnp.square(jax.nn.relu(h2 @ w_1)) @ w_2
    return x


def setup_inputs(seed: int = 0) -> dict:
    key = jax.random.key(seed)
    ks = jax.random.split(key, 20)
    f32 = jnp.float32
    nrm = lambda k, shape, s: (jax.random.normal(k, shape, f32) * s)
    return {
        "x": nrm(ks[0], (BATCH, SEQ, D_MODEL), 1.0),
        "g_mix": 1.0 + nrm(ks[1], (DEPTH, D_MODEL), 0.05),
        "w_in": nrm(ks[2], (DEPTH, D_MODEL, IN_COLS), D_MODEL ** -0.5),
        "b_f": FORGET_BIAS_INIT + nrm(ks[3], (DEPTH, FOX_HEADS), 0.1),
        "lam_q1": nrm(ks[4], (DEPTH, HEAD_DIM), 0.1),
        "lam_k1": nrm(ks[5], (DEPTH, HEAD_DIM), 0.1),
        "lam_q2": nrm(ks[6], (DEPTH, HEAD_DIM), 0.1),
        "lam_k2": nrm(ks[7], (DEPTH, HEAD_DIM), 0.1),
        "g_subln": 1.0 + nrm(ks[8], (DEPTH, DIFF_V_DIM), 0.05),
        "w_pa": nrm(ks[9], (DEPTH, DIFF_WIDTH, D_MODEL), DIFF_WIDTH ** -0.5),
        "w_pb": nrm(ks[10], (DEPTH, FOX_WIDTH, D_MODEL), FOX_WIDTH ** -0.5),
        "w_o": nrm(ks[11], (DEPTH, D_MODEL, D_MODEL), D_MODEL ** -0.5),
        "g_mlp": 1.0 + nrm(ks[12], (DEPTH, D_MODEL), 0.05),
        "w_1": nrm(ks[13], (DEPTH, D_MODEL, D_FF), D_MODEL ** -0.5),
        "w_2": nrm(ks[14], (DEPTH, D_FF, D_MODEL), D_FF ** -0.5),
        "rel_table": nrm(ks[15], (REL_BUCKETS, DIFF_HEADS), 0.5),
        "g_final": 1.0 + nrm(ks[16], (D_MODEL,), 0.05),
    }


def reference(x, g_mix, w_in, b_f, lam_q1, lam_k1, lam_q2, lam_k2, g_subln, w_pa, w_pb, w_o,
              g_mlp, w_1, w_2, rel_table, g_final):
    for l in range(DEPTH):
        x = hybrid_layer(x, l, g_mix[l], w_in[l], b_f[l], lam_q1[l], lam_k1[l], lam_q2[l],
                         lam_k2[l], g_subln[l], w_pa[l], w_pb[l], w_o[l], g_mlp[l], w_1[l],
                         w_2[l], rel_table)
    return rmsnorm(x, g_final)
```

```python
import math
from contextlib import ExitStack

import numpy as np
import concourse.bass as bass
import concourse.mybir as mybir
from concourse.bass_utils import run_bass_kernel_spmd

F32 = mybir.dt.float32
BF16 = mybir.dt.bfloat16
AF = mybir.ActivationFunctionType
ALU = mybir.AluOpType

NCORES = 8
SEQ = 2048
D = 1024
NSEQ = 2
NEG = -30000.0
EPS = 1e-6
NR = 5
COL_DQ, COL_DK, COL_DV, COL_FQ, COL_FK, COL_FV, COL_FL, COL_GL = 0, 512, 1024, 1536, 2048, 2560, 3072, 3080


class _Op:
    __slots__ = ("eng", "fn", "deps", "is_dma", "signal", "seq", "sem_key", "target", "prev_target", "idx")


class Prog:
    ENGS = ("pe", "act", "dve", "pool", "sp")
    DMA_POOLS = {"sp": 16, "pool": 16, "act": 4}

    def __init__(self):
        self.ops = []
        self.last_w = {}
        self.readers = {}
        self.pending_barrier = {}
        self.last_on = {}
        self.dma_rr = {q: 0 for q in self.DMA_POOLS}
        self.dma_tot = {}

    def ins(self, eng, method, reads, writes, *args, dma=False, **kwargs):
        return self.op(eng, (method, args, kwargs), reads, writes, dma)

    def op(self, eng, fn, reads=(), writes=(), dma=False):
        o = _Op()
        o.eng, o.fn, o.is_dma, o.signal, o.seq, o.idx = eng, fn, dma, False, 0, len(self.ops)
        deps = {}

        def add(d):
            if d is not None:
                deps[d.idx] = d

        for t in reads:
            add(self.last_w.get(t))
        for t in writes:
            add(self.last_w.get(t))
            r = self.readers.get(t)
            if r:
                for k, v in r.items():
                    if k == "dma":
                        for d in v:
                            add(d)
                    else:
                        add(v)
        pb = self.pending_barrier.pop(eng, None)
        if pb:
            for d in pb:
                add(d)
        dl = []
        for d in deps.values():
            if d.eng == "pe" and eng == "pe" and not d.is_dma and not dma:
                continue
            dl.append(d)
            if not d.is_dma:
                d.signal = True
        o.deps = dl
        if dma:
            n = self.DMA_POOLS[eng]
            i = self.dma_rr[eng] % n
            self.dma_rr[eng] += 1
            key = (eng, i)
            tot = self.dma_tot.get(key, 0)
            o.sem_key, o.prev_target, o.target = key, tot, tot + 16
            self.dma_tot[key] = tot + 16
        for t in writes:
            self.last_w[t] = o
            self.readers[t] = {}
        for t in reads:
            r = self.readers.setdefault(t, {})
            if dma:
                r.setdefault("dma", []).append(o)
            else:
                r[eng] = o
        self.ops.append(o)
        self.last_on[eng] = o
        return o

    def barrier(self):
        lasts = list(self.last_on.values())
        for e in self.ENGS:
            self.pending_barrier[e] = list(lasts)

    def emit(self, nc, es):
        sems = {e: es.enter_context(nc.semaphore("s_" + e)) for e in self.ENGS}
        dsem = {}
        for q, n in self.DMA_POOLS.items():
            for i in range(n):
                dsem[(q, i)] = es.enter_context(nc.semaphore("d_%s%d" % (q, i)))
        cnt = {e: 0 for e in self.ENGS}
        for o in self.ops:
            if not o.is_dma and o.signal:
                cnt[o.eng] += 1
                o.seq = cnt[o.eng]
        by_eng = {e: [o for o in self.ops if o.eng == e] for e in self.ENGS}
        dma_tot = self.dma_tot

        def body(e, E):
            known = {}

            def wait(key, val):
                if val > 0 and known.get(key, 0) < val:
                    E.wait_ge(sems[key] if isinstance(key, str) else dsem[key], val)
                    known[key] = val

            for o in by_eng[e]:
                for d in o.deps:
                    if d.is_dma:
                        wait(d.sem_key, d.target)
                    else:
                        wait(d.eng, d.seq)
                meth, a, kw = o.fn
                if o.is_dma:
                    wait(o.sem_key, o.prev_target)
                    getattr(E, meth)(*a, **kw).then_inc(dsem[o.sem_key], 16)
                else:
                    ins = getattr(E, meth)(*a, **kw)
                    if o.signal:
                        ins.then_inc(sems[e], 1)
            for (q, i), tot in dma_tot.items():
                if q == e:
                    wait((q, i), tot)

        block = es.enter_context(nc.Block())

        @block.tensor
        def _(E):
            body("pe", E)

        @block.scalar
        def _(E):
            body("act", E)

        @block.vector
        def _(E):
            body("dve", E)

        @block.gpsimd
        def _(E):
            body("pool", E)

        @block.sync
        def _(E):
            body("sp", E)


def build(nseq=NSEQ, stop_after=None, debug=False):
    nc = bass.Bass("TRN2", target_bir_lowering=False)
    P = Prog()
    I = P.ins

    def din(name, shape, dt=F32):
        return nc.dram_tensor(name, list(shape), dt, kind="ExternalInput").ap()

    x_d = din("x", [NSEQ * SEQ, D])
    w_in = din("w_in", [D, 5128])
    w_pa = din("w_pa", [512, D])
    w_pb = din("w_pb", [512, D])
    w_o = din("w_o", [D, D])
    w_1 = din("w_1", [D, 4096])
    w_2 = din("w_2", [4096, D])
    g_mix = din("g_mix", [D])
    g_mlp = din("g_mlp", [D])
    g_final = din("g_final", [D])
    g_subln = din("g_subln", [128])
    b_f = din("b_f", [8])
    lamv = din("lamv", [4 * 64])
    rel_t = din("rel_table", [32, 4])
    cst_d = din("cst", [128, 6 * 128])
    braw_d = din("bias_raw", [128, 4 * 2 * 128])
    out_d = nc.dram_tensor("out", [NSEQ * SEQ, D], F32, kind="ExternalOutput").ap()
    dbg_d = None
    dbg_map = {}
    dbg_off = [0]
    if debug:
        dbg_d = nc.dram_tensor("dbg", [128, 65536], F32, kind="ExternalOutput").ap()

    def sb(name, shape, dt):
        return nc.alloc_sbuf_tensor(name, list(shape), dt).ap()

    cstf = sb("cstf", [128, 6, 128], F32)
    ident_bf = sb("ident_bf", [128, 128], BF16)
    maskf_bf = sb("maskf_bf", [128, 128], BF16)
    bd = sb("bd", [128, 4, 2, 2, 128], BF16)
    gmix_c = sb("gmix_c", [128, 8], F32)
    gmlp_c = sb("gmlp_c", [128, 8], F32)
    gfin_b = sb("gfin_b", [128, 1024], F32)
    gs_b = sb("gs_b", [128, 128], F32)
    bf_b = sb("bf_b", [128, 8], F32)
    lam_b = sb("lam_b", [128, 4, 64], F32)
    t15 = sb("t15", [128, 4], F32)
    misc = sb("misc", [128, 32], F32)
    wfl = sb("wfl", [128, 8, 8], BF16)
    ring = sb("ring", [128, NR, 4096], BF16)
    xring = sb("xring", [128, 3, 1024], F32)
    HO = sb("HO", [128, 4, 4096], F32)
    OVB = 78464
    OV = sb("OV", [128, OVB // 4], F32)
    psum = nc.alloc_psum_tensor("psum", [128, 8, 512], F32).ap()

    def hT(tg):
        return HO[:, tg, 0:2048].bitcast(BF16).rearrange("p (k n) -> p k n", n=512)

    def oT(tg):
        return HO[:, tg, 2048:4096].bitcast(BF16).rearrange("p (k n) -> p k n", n=512)

    def x1(tg):
        return HO[:, tg, :].rearrange("p (t n) -> p t n", n=1024)

    class Carver:
        def __init__(self):
            self.off = 0

        def take(self, nbytes):
            o = self.off
            self.off += (nbytes + 31) // 32 * 32
            assert self.off <= OVB, self.off
            return o

        def f32(self, n):
            o = self.take(n * 4)
            return OV[:, o // 4:o // 4 + n]

        def bf16(self, n):
            o = self.take(n * 2)
            return OV[:, o // 4:o // 4 + (n + 1) // 2].bitcast(BF16)[:, 0:n]

    c1 = Carver()
    Vd = c1.bf16(16 * 4 * 129).rearrange("p (j h e) -> p j h e", j=16, h=4)
    Vf = c1.bf16(16 * 8 * 65).rearrange("p (j h e) -> p j h e", j=16, h=8)
    qT = c1.bf16(2 * 2048).rearrange("p (m n) -> p m n", m=2)
    kTz = c1.bf16(2 * 2048).rearrange("p (m n) -> p m n", m=2)
    ET = c1.bf16(3 * 2 * 512).rearrange("p (b m n) -> p b m n", b=3, m=2)
    xn = c1.bf16(2 * 1024).rearrange("p (b n) -> p b n", b=2)
    junk = c1.bf16(1024)
    fz = c1.f32(5 * 128).rearrange("p (a n) -> p a n", a=5)
    Csb = c1.f32(128).rearrange("p (h j) -> p h j", h=8)
    Rsb = c1.f32(128).rearrange("p (h j) -> p h j", h=8)
    ft0 = c1.f32(2 * 128).rearrange("p (b n) -> p b n", b=2)
    ftt = c1.f32(2 * 128).rearrange("p (b n) -> p b n", b=2)
    obf = c1.bf16(2 * 128).rearrange("p (b n) -> p b n", b=2)
    sm1 = c1.f32(64)
    c0 = Carver()
    braw = c0.f32(4 * 2 * 128).rearrange("p (h t n) -> p h t n", h=4, t=2)
    bfull = c0.f32(128)
    c2 = Carver()
    h2T = c2.bf16(8 * 2048).rearrange("p (k n) -> p k n", k=8)
    aT = c2.bf16(2 * 4 * 512).rearrange("p (b k n) -> p b k n", b=2, k=4)
    mT = c2.bf16(8 * 512).rearrange("p (k n) -> p k n", k=8)
    sg = c2.f32(2 * 2 * 512).rearrange("p (a b n) -> p a b n", a=2, b=2)
    m01 = c2.f32(2 * 2 * 512).rearrange("p (a b n) -> p a b n", a=2, b=2)
    rr = c2.f32(2 * 512).rearrange("p (b n) -> p b n", b=2)
    xn2 = c2.bf16(2 * 1024).rearrange("p (b n) -> p b n", b=2)
    junk2 = c2.bf16(1024)
    sm2 = c2.f32(64)

    def ps(b):
        return psum[:, b, :]

    def ps_bf(b):
        return psum[:, b, :].bitcast(BF16)

    def dump(name, ap, reads):
        if not debug:
            return
        shp = list(ap.shape)
        n = 1
        for s_ in shp[1:]:
            n *= s_
        o = dbg_off[0]
        dbg_off[0] += n
        assert dbg_off[0] <= 65536
        dbg_map[name] = (o, shp)
        dst = dbg_d[0:shp[0], o:o + n]
        if len(shp) == 3:
            dst = dst.rearrange("p (a b) -> p a b", a=shp[1])
        elif len(shp) == 4:
            dst = dst.rearrange("p (a b c) -> p a b c", a=shp[1], b=shp[2])
        elif len(shp) == 5:
            dst = dst.rearrange("p (a b c d) -> p a b c d", a=shp[1], b=shp[2], c=shp[3])
        I("pool", "dma_start", reads, [("dbg", name)], out=dst, in_=ap, dma=True)

    def win_cols(c0, n):
        return w_in[:, c0:c0 + n].rearrange("(k p) n -> p k n", p=128)

    def kp(ap):
        return ap.rearrange("(k p) n -> p k n", p=128)

    def unit_cols(u):
        if u < 4:
            return COL_DQ + u * 128, COL_DK + u * 128
        return COL_FQ + (u - 4) * 128, COL_FK + (u - 4) * 128

    sched = []
    for s in range(nseq):
        sched.append(((s, "dv"), [(0, 8, 512, win_cols(COL_DV, 512))]))
        sched.append(((s, "fv"), [(0, 8, 512, win_cols(COL_FV, 512))]))
        for u in range(8):
            cq, ck = unit_cols(u)
            sched.append(((s, "u", u), [(0, 8, 128, win_cols(cq, 128)), (1024, 8, 128, win_cols(ck, 128))]))
        for tg in range(4):
            for nm in ("g0", "g2", "pl", "g1", "g3", "ph", "wo0", "wo1"):
                if nm[0] == "g":
                    parts = [(0, 8, 512, win_cols(COL_GL + 512 * int(nm[1]), 512))]
                elif nm == "pl":
                    parts = [(0, 4, 512, kp(w_pa[:, 0:512])), (2048, 4, 512, kp(w_pb[:, 0:512]))]
                elif nm == "ph":
                    parts = [(0, 4, 512, kp(w_pa[:, 512:1024])), (2048, 4, 512, kp(w_pb[:, 512:1024]))]
                else:
                    mh = int(nm[2])
                    parts = [(0, 8, 512, kp(w_o[:, mh * 512:(mh + 1) * 512]))]
                sched.append(((s, tg, nm), parts))
        for e8 in range(8):
            sched.append(((s, "w1", e8), [(0, 8, 512, kp(w_1[:, e8 * 512:(e8 + 1) * 512]))]))
            sched.append(((s, "w2", e8), [(0, 4, 1024, kp(w_2[e8 * 512:(e8 + 1) * 512, :]))]))
    sched_idx = {k: i for i, (k, _) in enumerate(sched)}
    wstate = {"issued": 0, "want": 0}
    released = set()
    LOOKAHEAD = NR - 1

    def wpump():
        while wstate["issued"] < min(wstate["want"], len(sched)):
            i = wstate["issued"]
            if i >= NR and (i - NR) not in released:
                break
            slot = i % NR
            for (doff, k, n, src) in sched[i][1]:
                dst = ring[:, slot, doff:doff + k * n].rearrange("p (k n) -> p k n", k=k)
                I("pool", "dma_start", [], [("ring", slot)], out=dst, in_=src, dma=True)
            wstate["issued"] += 1

    def wget(key):
        idx = sched_idx[key]
        wstate["want"] = max(wstate["want"], idx + 1 + LOOKAHEAD)
        wpump()
        assert wstate["issued"] > idx, key
        return idx % NR, ("ring", idx % NR)

    def wdone(key):
        released.add(sched_idx[key])
        wpump()

    def wview(slot, doff, k, n):
        return ring[:, slot, doff:doff + k * n].rearrange("p (k n) -> p k n", k=k)

    I("sp", "dma_start", [], ["cstf"], out=cstf.rearrange("p a n -> p (a n)"), in_=cst_d, dma=True)
    I("sp", "dma_start", [], ["braw"], out=braw.rearrange("p h t n -> p (h t n)"), in_=braw_d, dma=True)
    I("sp", "dma_start", [], ["gmix_c"], out=gmix_c, in_=g_mix.rearrange("(c p) -> p c", p=128), dma=True)
    I("sp", "dma_start", [], ["gmlp_c"], out=gmlp_c, in_=g_mlp.rearrange("(c p) -> p c", p=128), dma=True)
    I("sp", "dma_start", [], ["gfin_b"], out=gfin_b, in_=g_final.partition_broadcast(128), dma=True)
    I("sp", "dma_start", [], ["gs_b"], out=gs_b, in_=g_subln.partition_broadcast(128), dma=True)
    I("sp", "dma_start", [], ["bf_b"], out=bf_b, in_=b_f.partition_broadcast(128), dma=True)
    I("sp", "dma_start", [], ["lam_b"], out=lam_b.rearrange("p a n -> p (a n)"), in_=lamv.partition_broadcast(128), dma=True)
    I("sp", "dma_start", [], ["t15"], out=t15, in_=rel_t[15, :].partition_broadcast(128), dma=True)
    I("pool", "dma_start", [], ["wfl"], out=wfl, in_=win_cols(COL_FL, 8), dma=True)

    I("pool", "memset", [], ["misc0"], misc[:, 0:1], EPS)
    I("pool", "memset", [], ["misc6"], misc[:, 6:7], 1.0)
    I("dve", "tensor_copy", ["cstf"], ["ident_bf"], out=ident_bf, in_=cstf[:, 0, :])
    I("dve", "tensor_copy", ["cstf"], ["maskf_bf"], out=maskf_bf, in_=cstf[:, 3, :])
    I("dve", "tensor_scalar", ["gs_b"], ["gs_b"], out=gs_b, in0=gs_b, scalar1=0.8, scalar2=None, op0=ALU.mult)
    I("dve", "tensor_tensor", ["lam_b"], ["bfull"], out=bfull[:, 0:64], in0=lam_b[:, 0, :], in1=lam_b[:, 1, :], op=ALU.mult)
    I("dve", "tensor_scalar", ["bfull"], ["bfull", "misc1"], out=bfull[:, 0:64], in0=bfull[:, 0:64], scalar1=1.0, scalar2=None,
      op0=ALU.mult, op1=ALU.add, accum_out=misc[:, 1:2])
    I("dve", "tensor_tensor", ["lam_b", "bfull"], ["bfull"], out=bfull[:, 64:128], in0=lam_b[:, 2, :], in1=lam_b[:, 3, :], op=ALU.mult)
    I("dve", "tensor_scalar", ["bfull"], ["bfull", "misc2"], out=bfull[:, 64:128], in0=bfull[:, 64:128], scalar1=1.0, scalar2=None,
      op0=ALU.mult, op1=ALU.add, accum_out=misc[:, 2:3])
    I("act", "activation", ["misc1", "misc2"], ["misc34"], out=misc[:, 3:5], in_=misc[:, 1:3], func=AF.Exp)
    I("dve", "tensor_tensor", ["misc34"], ["misc5"], out=misc[:, 5:6], in0=misc[:, 3:4], in1=misc[:, 4:5], op=ALU.subtract)
    I("dve", "tensor_scalar", ["misc5"], ["misc5"], out=misc[:, 5:6], in0=misc[:, 5:6], scalar1=0.2, scalar2=-1.0, op0=ALU.add, op1=ALU.mult)
    for h in range(4):
        for ty in range(2):
            if ty == 0:
                I("dve", "scalar_tensor_tensor", ["braw", "t15", "cstf", "bd"], ["bfull"], out=bfull, in0=braw[:, h, ty, :], scalar=t15[:, h:h + 1],
                  in1=cstf[:, 5, :], op0=ALU.subtract, op1=ALU.add)
            else:
                I("dve", "tensor_scalar", ["braw", "t15", "bd"], ["bfull"], out=bfull, in0=braw[:, h, ty, :], scalar1=t15[:, h:h + 1],
                  scalar2=None, op0=ALU.subtract)
            I("dve", "tensor_copy", ["bfull"], ["bd_hi"], out=bd[:, h, ty, 0, :], in_=bfull)
            I("dve", "tensor_tensor", ["bfull", "bd_hi"], ["bd"], out=bd[:, h, ty, 1, :], in0=bfull, in1=bd[:, h, ty, 0, :], op=ALU.subtract)
    dump("bd", bd, ["bd"])
    dump("misc", misc, ["misc5"])

    mm_rot = {"i": 0}
    B03 = [0, 1, 2, 3]
    B47 = [4, 5, 6, 7]
    B07 = list(range(8))

    def nbank(allowed):
        b = allowed[mm_rot["i"] % len(allowed)]
        mm_rot["i"] += 1
        return b

    def rstd_chain(ss_ap, out_ap, scale, toks_in, tok_out, tmp_ap):
        I("act", "activation", list(toks_in) + ["misc0"], [("tmp", tok_out)], out=tmp_ap, in_=ss_ap, func=AF.Ln, bias=misc[:, 0:1], scale=scale)
        I("act", "activation", [("tmp", tok_out)], [tok_out], out=out_ap, in_=tmp_ap, func=AF.Exp, scale=-0.5)

    done = {"stop": False}

    def stage(name):
        if stop_after == name:
            done["stop"] = True
        return done["stop"]

    def hT_toks(tg):
        return [("hT", tg, j) for j in range(4)]

    def oT_toks(tg):
        return [("oT", tg, u, qq) for u in range(8) for qq in range(4)]

    def attention_group(s, u, ub, g, et_rr):
        is_diff = u < 4
        width = 129 if is_diff else 65
        hd = 128 if is_diff else 64
        nkb = 4 * g + 4
        sbuf_i = {"n": 0}
        pend = []

        def acc_ap(a):
            bank = 4 + a // 2
            c = (a % 2) * 129
            return psum[:, bank, c:c + width], bank

        def do_qk(kb):
            j = kb - 4 * g
            c0 = max(0, j) * 128
            pair = sbuf_i["n"] % 2
            sbuf_i["n"] += 1
            eb = et_rr["i"] % 3
            et_rr["i"] += 1
            tgk = kb // 4
            for m in range(2):
                bk = pair * 2 + m
                extra = []
                if is_diff:
                    for qq in range(max(0, j), 4):
                        Q = 4 * g + qq
                        if Q == kb:
                            extra.append((qq, 0))
                        elif Q == kb + 1:
                            extra.append((qq, 1))
                elif j >= 0:
                    extra.append((j, -1))
                nex = len(extra) * (2 if is_diff else 1)
                I("pe", "matmul", [("kT", tgk, m), "kTz0", ("qT", g, m), "qT0"], [("ps", bk)],
                  ps(bk)[:, c0:512], lhsT=kTz[:, m, kb * 128:(kb + 1) * 128], rhs=qT[:, m, g * 512 + c0:(g + 1) * 512],
                  start=True, stop=(nex == 0))
                k_ex = 0
                for (qq, ty) in extra:
                    if is_diff:
                        for hl in range(2):
                            k_ex += 1
                            I("pe", "matmul", ["ident_bf", "bd"], [("ps", bk)], ps(bk)[:, qq * 128:(qq + 1) * 128], lhsT=ident_bf,
                              rhs=bd[:, u, ty, hl, :], start=False, stop=(k_ex == nex))
                    else:
                        k_ex += 1
                        I("pe", "matmul", ["ident_bf", "maskf_bf"], [("ps", bk)], ps(bk)[:, qq * 128:(qq + 1) * 128], lhsT=ident_bf,
                          rhs=maskf_bf, start=False, stop=(k_ex == nex))
            if is_diff:
                I("act", "activation", [("ps", pair * 2), ("ps", pair * 2 + 1)], [("ET", eb, 0), ("ET", eb, 1)],
                  out=ET[:, eb, :, c0:512], in_=psum[:, pair * 2:pair * 2 + 2, c0:512], func=AF.Exp)
            else:
                for m in range(2):
                    hh = 2 * (u - 4) + m
                    I("act", "activation", [("ps", pair * 2 + m), "Csb"], [("ET", eb, m)],
                      out=ET[:, eb, m, c0:512], in_=psum[:, pair * 2 + m, c0:512], func=AF.Exp, bias=Csb[:, hh, kb:kb + 1], scale=1.0)
            pend.append((kb, eb))

        def finalize(qq):
            fb = qq % 2
            a0, a1 = qq * 2, qq * 2 + 1
            p0, _ = acc_ap(a0)
            p1, _ = acc_ap(a1)
            rc = sm1[:, 16 + 4 * fb:18 + 4 * fb]
            nl = sm1[:, 18 + 4 * fb:19 + 4 * fb]
            ssq = sm1[:, 24 + fb:25 + fb]
            rsd = sm1[:, 26 + fb:27 + fb]
            tmp = sm1[:, 28 + fb:29 + fb]
            I("dve", "reciprocal", [("accb", qq)], [("rc", fb, 0)], out=rc[:, 0:1], in_=p0[:, hd:hd + 1])
            I("dve", "reciprocal", [("accb", qq)], [("rc", fb, 1)], out=rc[:, 1:2], in_=p1[:, hd:hd + 1])
            if is_diff:
                I("dve", "tensor_tensor", [("rc", fb, 1), "misc5"], [("nl", fb)], out=nl, in0=rc[:, 1:2], in1=misc[:, 5:6], op=ALU.mult)
                I("dve", "tensor_scalar", [("accb", qq), ("rc", fb, 0)], [("ft0", fb)], out=ft0[:, fb, :], in0=p0[:, 0:128], scalar1=rc[:, 0:1],
                  scalar2=None, op0=ALU.mult)
                I("dve", "scalar_tensor_tensor", [("accb", qq), ("nl", fb), ("ft0", fb)], [("ftt", fb)], out=ftt[:, fb, :], in0=p1[:, 0:128],
                  scalar=nl, in1=ft0[:, fb, :], op0=ALU.mult, op1=ALU.add)
                I("dve", "tensor_tensor", [("ftt", fb), ("ft0", fb)], [("ft0", fb)], out=ft0[:, fb, :], in0=ftt[:, fb, :], in1=ftt[:, fb, :],
                  op=ALU.mult)
                I("dve", "tensor_scalar", [("ft0", fb)], [("ft0", fb), ("ssq", fb)], out=ft0[:, fb, :], in0=ft0[:, fb, :], scalar1=1.0, scalar2=None,
                  op0=ALU.mult, op1=ALU.add, accum_out=ssq)
                rstd_chain(ssq, rsd, 1.0 / 128, [("ssq", fb)], ("rsd", fb), tmp)
                I("dve", "scalar_tensor_tensor", [("ftt", fb), ("rsd", fb), "gs_b"], [("obf", fb)], out=obf[:, fb, :], in0=ftt[:, fb, :],
                  scalar=rsd, in1=gs_b, op0=ALU.mult, op1=ALU.mult)
            else:
                I("dve", "tensor_scalar", [("accb", qq), ("rc", fb, 0)], [("obf", fb)], out=obf[:, fb, 0:64], in0=p0[:, 0:64], scalar1=rc[:, 0:1],
                  scalar2=None, op0=ALU.mult)
                I("dve", "tensor_scalar", [("accb", qq), ("rc", fb, 1), ("obf", fb)], [("obf", fb)], out=obf[:, fb, 64:128], in0=p1[:, 0:64],
                  scalar1=rc[:, 1:2], scalar2=None, op0=ALU.mult)
            I("pe", "transpose", [("obf", fb), "ident_bf"], [("accb", qq)], out=ps_bf(4 + qq)[:, 0:128], in_=obf[:, fb, :], identity=ident_bf)
            I("dve", "tensor_copy", [("accb", qq)], [("oT", g, u, qq)], out=oT(g)[:, u, qq * 128:(qq + 1) * 128], in_=ps_bf(4 + qq)[:, 0:128])

        def do_pv(kb, eb):
            j = kb - 4 * g
            seen = set()
            for qq in range(max(0, j), 4):
                for m in range(2):
                    a = qq * 2 + m
                    oap, bank = acc_ap(a)
                    if is_diff:
                        rhs = Vd[:, kb, u, :]
                        vt = [("Vd", kb), "Vd_ones"]
                    else:
                        rhs = Vf[:, kb, 2 * (u - 4) + m, :]
                        vt = [("Vf", kb), "Vf_ones"]
                    st = (kb == 0 and bank not in seen)
                    seen.add(bank)
                    I("pe", "matmul", [("ET", eb, m)] + vt, [("accb", qq)], oap, lhsT=ET[:, eb, m, qq * 128:(qq + 1) * 128], rhs=rhs,
                      start=st, stop=(kb == 4 * g + qq), skip_group_check=True)
            if j >= 0:
                finalize(j)

        do_qk(0)
        for kb in range(nkb):
            if kb + 1 < nkb:
                do_qk(kb + 1)
            kbp, ebp = pend.pop(0)
            do_pv(kbp, ebp)

    for s in range(nseq):
        if done["stop"]:
            break
        row0 = s * SEQ
        s_dv, t_dv = wget((s, "dv"))
        P.barrier()
        I("pool", "memset", [], ["Vd_ones"], Vd[:, :, :, 128:129], 1.0)
        I("pool", "memset", [], ["Vf_ones"], Vf[:, :, :, 64:65], 1.0)
        I("pool", "memset", [], ["kTz0"], kTz[64:128, 0, :], 0.0)
        I("pool", "memset", [], ["kTz0"], kTz[0:64, 1, :], 0.0)
        I("pool", "memset", [], ["qT0"], qT[64:128, 0, :], 0.0)
        I("pool", "memset", [], ["qT0"], qT[0:64, 1, :], 0.0)

        def xload(i):
            slot = i % 3
            I("sp", "dma_start", [], [("xr", slot)], out=xring[:, slot, :], in_=x_d[row0 + i * 128:row0 + (i + 1) * 128, :], dma=True)

        xload(0)
        xload(1)
        for i in range(16):
            if i + 2 < 16:
                xload(i + 2)
            slot = i % 3
            b2 = i % 2
            tg, off = i // 4, (i % 4) * 128
            ssa = sm1[:, b2:b2 + 1]
            rsa = sm1[:, 4 + b2:5 + b2]
            tma = sm1[:, 8 + b2:9 + b2]
            I("act", "activation", [("xr", slot)], ["junk", ("ssA", b2)], out=junk, in_=xring[:, slot, :], func=AF.Square, accum_out=ssa)
            rstd_chain(ssa, rsa, 1.0 / D, [("ssA", b2)], ("rsA", b2), tma)
            I("pool", "tensor_scalar", [("xr", slot), ("rsA", b2)], [("xn", b2)], out=xn[:, b2, :], in0=xring[:, slot, :], scalar1=rsa, scalar2=1.0,
              op0=ALU.mult, op1=ALU.mult)
            bk = nbank(B03)
            for c in range(8):
                I("pe", "transpose", [("xn", b2), "ident_bf"], [("ps", bk)], out=ps_bf(bk)[:, c * 128:(c + 1) * 128],
                  in_=xn[:, b2, c * 128:(c + 1) * 128], identity=ident_bf)
            I("dve", "tensor_tensor", [("ps", bk), "gmix_c"], [("hT", tg, i % 4)], out=hT(tg)[:, :, off:off + 128],
              in0=ps_bf(bk).rearrange("p (c n) -> p c n", c=8), in1=gmix_c.unsqueeze(2).to_broadcast([128, 8, 128]), op=ALU.mult)
        if debug and s == 0:
            for tg in range(4):
                dump("hT%d" % tg, hT(tg), hT_toks(tg))
        if stage("A"):
            break

        for (wkey, dst, nh, hd, vtok) in (("dv", Vd, 4, 128, "Vd"), ("fv", Vf, 8, 64, "Vf")):
            slot_w, tok_w = wget((s, wkey))
            wv = wview(slot_w, 0, 8, 512)
            for i in range(16):
                tg, off = i // 4, (i % 4) * 128
                bk = nbank(B03)
                for kc in range(8):
                    I("pe", "matmul", [("hT", tg, i % 4), tok_w], [("ps", bk)], ps(bk), lhsT=hT(tg)[:, kc, off:off + 128], rhs=wv[:, kc, :],
                      start=(kc == 0), stop=(kc == 7))
                src = ps(bk).rearrange("p (h e) -> p h e", h=nh)
                if i % 2 == 0:
                    I("act", "activation", [("ps", bk)], [(vtok, i)], out=dst[:, i, :, 0:hd], in_=src, func=AF.Copy)
                else:
                    I("dve", "tensor_copy", [("ps", bk)], [(vtok, i)], out=dst[:, i, :, 0:hd], in_=src)
            wdone((s, wkey))
        wget((s, "u", 0))
        bkf = nbank(B03)
        for i in range(16):
            tg, off = i // 4, (i % 4) * 128
            for kc in range(8):
                I("pe", "matmul", [("hT", tg, i % 4), "wfl"], [("ps", bkf)], ps(bkf)[:, i * 8:(i + 1) * 8], lhsT=hT(tg)[:, kc, off:off + 128],
                  rhs=wfl[:, kc, :], start=(kc == 0), stop=(kc == 7), skip_group_check=True)
        zt, Lt, incl, excl = fz[:, 0, :], fz[:, 1, :], fz[:, 2, :], fz[:, 3, :]
        I("dve", "tensor_tensor", [("ps", bkf), "bf_b"], ["fz_z"], out=zt.rearrange("p (j h) -> p j h", h=8),
          in0=ps(bkf)[:, 0:128].rearrange("p (j h) -> p j h", h=8), in1=bf_b.unsqueeze(1).to_broadcast([128, 16, 8]), op=ALU.add)
        I("act", "activation", ["fz_z"], ["fz_e"], out=fz[:, 4, :], in_=zt, func=AF.Exp, scale=-1.0)
        I("act", "activation", ["fz_e", "misc6"], ["fz_L"], out=Lt.rearrange("p (h j) -> p j h", h=8),
          in_=fz[:, 4, :].rearrange("p (j h) -> p j h", h=8), func=AF.Ln, bias=misc[:, 6:7], scale=1.0)
        I("dve", "tensor_tensor_scan", ["fz_L", "cstf"], ["fz_incl"], out=incl, data0=cstf[:, 4, :], data1=Lt, initial=0.0, op0=ALU.mult, op1=ALU.add)
        I("dve", "tensor_tensor", ["fz_incl", "fz_L"], ["fz_excl"], out=excl, in0=incl, in1=Lt, op=ALU.subtract)
        bkc = nbank(B03)
        I("pe", "matmul", ["fz_L", "cstf"], [("ps", bkc)], ps(bkc)[:, 0:128], lhsT=cstf[:, 1, :], rhs=Lt, start=True, stop=False)
        I("pe", "matmul", ["fz_excl", "cstf"], [("ps", bkc)], ps(bkc)[:, 0:128], lhsT=cstf[:, 2, :], rhs=excl, start=False, stop=True)
        bkr = nbank(B03)
        I("pe", "matmul", ["fz_incl", "cstf"], [("ps", bkr)], ps(bkr)[:, 0:128], lhsT=cstf[:, 2, :], rhs=incl, start=True, stop=True)
        I("act", "activation", [("ps", bkc)], ["Csb"], out=Csb.rearrange("p h j -> p (h j)"), in_=ps(bkc)[:, 0:128], func=AF.Copy)
        I("dve", "tensor_copy", [("ps", bkr)], ["Rsb"], out=Rsb.rearrange("p h j -> p (h j)"), in_=ps(bkr)[:, 0:128])
        if debug and s == 0:
            dump("Vd", Vd[:, 0:2], [("Vd", 0), ("Vd", 1), "Vd_ones"])
            dump("Vf", Vf[:, 0:2], [("Vf", 0), ("Vf", 1), "Vf_ones"])
            dump("Csb", Csb, ["Csb"])
            dump("Rsb", Rsb, ["Rsb"])
        if stage("B"):
            break

        et_rr = {"i": 0}
        for u in range(8):
            ub = u % 2
            slot_w, tok_w = wget((s, "u", u))
            wq = wview(slot_w, 0, 8, 128)
            wk = wview(slot_w, 1024, 8, 128)
            for tg in range(4):
                bk = nbank(B03)
                for kc in range(8):
                    I("pe", "matmul", hT_toks(tg) + [tok_w], [("ps", bk)], ps(bk), lhsT=wq[:, kc, :], rhs=hT(tg)[:, kc, :],
                      start=(kc == 0), stop=(kc == 7))
                I("act", "activation", [("ps", bk), "qT0"], [("qT", tg, 0)], out=qT[0:64, 0, tg * 512:(tg + 1) * 512], in_=ps(bk)[0:64, :],
                  func=AF.Copy, scale=0.125)
                I("act", "activation", [("ps", bk), "qT0"], [("qT", tg, 1)], out=qT[64:128, 1, tg * 512:(tg + 1) * 512], in_=ps(bk)[64:128, :],
                  func=AF.Copy, scale=0.125)
                bk = nbank(B03)
                for kc in range(8):
                    I("pe", "matmul", hT_toks(tg) + [tok_w], [("ps", bk)], ps(bk), lhsT=wk[:, kc, :], rhs=hT(tg)[:, kc, :],
                      start=(kc == 0), stop=(kc == 7))
                I("dve", "tensor_copy", [("ps", bk), "kTz0"], [("kT", tg, 0)], out=kTz[0:64, 0, tg * 512:(tg + 1) * 512], in_=ps(bk)[0:64, :])
                I("dve", "tensor_copy", [("ps", bk), "kTz0"], [("kT", tg, 1)], out=kTz[64:128, 1, tg * 512:(tg + 1) * 512], in_=ps(bk)[64:128, :])
            if u == 4:
                I("pool", "memset", [], ["kTz0"], kTz[64:65, 0, :], 1.0)
                I("pool", "memset", [], ["kTz0"], kTz[0:1, 1, :], 1.0)
            if u >= 4:
                for m in range(2):
                    hh = 2 * (u - 4) + m
                    r0 = 64 if m == 0 else 0
                    I("dve", "tensor_scalar", ["Rsb", "qT0"], ["qT0"], out=qT[r0:r0 + 1, m, :].rearrange("p (j n) -> p j n", n=128),
                      in0=Rsb[r0:r0 + 1, hh, :].unsqueeze(2).to_broadcast([1, 16, 128]), scalar1=-1.0, scalar2=None, op0=ALU.mult)
            wdone((s, "u", u))
            if debug and s == 0 and u in (0, 4):
                dump("qT_u%d" % u, qT[:, :, 0:512], [("qT", 0, 0), ("qT", 0, 1), "qT0"])
                dump("kT_u%d" % u, kTz[:, :, 0:256], [("kT", 0, 0), ("kT", 0, 1), "kTz0"])
            for g in range(4):
                attention_group(s, u, ub, g, et_rr)
            if debug and s == 0 and u in (0, 4):
                dump("oT_u%d" % u, oT(0)[:, u, :], [("oT", 0, u, qq) for qq in range(4)])
            if stage("C%d" % u):
                break
        if done["stop"]:
            break

        wget((s, 0, "g0"))
        P.barrier()
        for tg in range(4):
            wg = {}
            wg[0] = wget((s, tg, "g0"))
            wg[2] = wget((s, tg, "g2"))
            s_pa, t_pa = wget((s, tg, "pl"))
            t_pb = t_pa
            wpa = wview(s_pa, 0, 4, 512)
            wpb = wview(s_pa, 2048, 4, 512)
            for n_ in range(8):
                if n_ == 4:
                    wdone((s, tg, "g0"))
                    wdone((s, tg, "g2"))
                    wdone((s, tg, "pl"))
                    wg[1] = wget((s, tg, "g1"))
                    wg[3] = wget((s, tg, "g3"))
                    s_pa, t_pa = wget((s, tg, "ph"))
                    t_pb = t_pa
                    wpa = wview(s_pa, 0, 4, 512)
                    wpb = wview(s_pa, 2048, 4, 512)
                s0, t0_ = wg[n_ // 4]
                s1, t1_ = wg[2 + n_ // 4]
                co = (n_ % 4) * 128
                w0 = wview(s0, 0, 8, 512)
                w1v = wview(s1, 0, 8, 512)
                pb2 = n_ % 2
                bG0, bG1, bA, bB = nbank(B07), nbank(B07), nbank(B07), nbank(B07)
                for kc in range(8):
                    I("pe", "matmul", hT_toks(tg) + [t0_], [("ps", bG0)], ps(bG0), lhsT=w0[:, kc, co:co + 128], rhs=hT(tg)[:, kc, :],
                      start=(kc == 0), stop=(kc == 7))
                for kc in range(8):
                    I("pe", "matmul", hT_toks(tg) + [t1_], [("ps", bG1)], ps(bG1), lhsT=w1v[:, kc, co:co + 128], rhs=hT(tg)[:, kc, :],
                      start=(kc == 0), stop=(kc == 7))
                for fc in range(4):
                    I("pe", "matmul", oT_toks(tg) + [t_pa], [("ps", bA)], ps(bA), lhsT=wpa[:, fc, co:co + 128], rhs=oT(tg)[:, fc, :],
                      start=(fc == 0), stop=(fc == 3))
                for fc in range(4):
                    I("pe", "matmul", oT_toks(tg) + [t_pb], [("ps", bB)], ps(bB), lhsT=wpb[:, fc, co:co + 128], rhs=oT(tg)[:, 4 + fc, :],
                      start=(fc == 0), stop=(fc == 3))
                I("act", "activation", [("ps", bG0)], [("sg", pb2, 0)], out=sg[:, pb2, 0, :], in_=ps(bG0), func=AF.Sigmoid)
                I("act", "activation", [("ps", bG1)], [("sg", pb2, 1)], out=sg[:, pb2, 1, :], in_=ps(bG1), func=AF.Sigmoid)
                I("dve", "tensor_tensor", [("sg", pb2, 0), ("ps", bA)], [("m01", pb2, 0)], out=m01[:, pb2, 0, :], in0=sg[:, pb2, 0, :], in1=ps(bA), op=ALU.mult)
                I("dve", "tensor_tensor", [("sg", pb2, 1), ("ps", bB)], [("m01", pb2, 1)], out=m01[:, pb2, 1, :], in0=sg[:, pb2, 1, :], in1=ps(bB), op=ALU.mult)
                I("pool", "tensor_tensor", [("m01", pb2, 0), ("m01", pb2, 1)], [("mT", n_)], out=mT[:, n_, :], in0=m01[:, pb2, 0, :], in1=m01[:, pb2, 1, :],
                  op=ALU.add)
            wdone((s, tg, "g1"))
            wdone((s, tg, "g3"))
            wdone((s, tg, "ph"))
            wo_s = [wget((s, tg, "wo0")), wget((s, tg, "wo1"))]
            for tt in range(4):
                i = tg * 4 + tt
                slot = (16 + i) % 3
                I("sp", "dma_start", [], [("xr", slot)], out=xring[:, slot, :], in_=x_d[row0 + i * 128:row0 + (i + 1) * 128, :], dma=True)
                for mh in range(2):
                    so, to = wo_s[mh]
                    wov = wview(so, 0, 8, 512)
                    bk = nbank(B07)
                    for n_ in range(8):
                        I("pe", "matmul", [("mT", n_), to], [("ps", bk)], ps(bk), lhsT=mT[:, n_, tt * 128:(tt + 1) * 128], rhs=wov[:, n_, :],
                          start=(n_ == 0), stop=(n_ == 7))
                    I("dve", "tensor_tensor", [("ps", bk), ("xr", slot)], hT_toks(tg) + oT_toks(tg) + [("x1", tg, tt, mh)],
                      out=x1(tg)[:, tt, mh * 512:(mh + 1) * 512], in0=ps(bk), in1=xring[:, slot, mh * 512:(mh + 1) * 512], op=ALU.add)
            wdone((s, tg, "wo0"))
            wdone((s, tg, "wo1"))
            for tt in range(4):
                b2 = tt % 2
                ssa = sm2[:, b2:b2 + 1]
                rsa = sm2[:, 4 + b2:5 + b2]
                tma = sm2[:, 8 + b2:9 + b2]
                x1t = [("x1", tg, tt, 0), ("x1", tg, tt, 1)]
                I("act", "activation", x1t, ["junk2", ("ssF", b2)], out=junk2, in_=x1(tg)[:, tt, :], func=AF.Square, accum_out=ssa)
                rstd_chain(ssa, rsa, 1.0 / D, [("ssF", b2)], ("rsF", b2), tma)
                I("pool", "tensor_scalar", x1t + [("rsF", b2)], [("xn2", b2)], out=xn2[:, b2, :], in0=x1(tg)[:, tt, :], scalar1=rsa, scalar2=1.0,
                  op0=ALU.mult, op1=ALU.mult)
                bk = nbank(B07)
                for c in range(8):
                    I("pe", "transpose", [("xn2", b2), "ident_bf"], [("ps", bk)], out=ps_bf(bk)[:, c * 128:(c + 1) * 128],
                      in_=xn2[:, b2, c * 128:(c + 1) * 128], identity=ident_bf)
                I("dve", "tensor_tensor", [("ps", bk), "gmlp_c"], [("h2T", tg, tt)], out=h2T[:, :, tg * 512 + tt * 128:tg * 512 + (tt + 1) * 128],
                  in0=ps_bf(bk).rearrange("p (c n) -> p c n", c=8), in1=gmlp_c.unsqueeze(2).to_broadcast([128, 8, 128]), op=ALU.mult)
        if debug and s == 0:
            dump("x1_0", x1(0)[:, 0, :], [("x1", 0, 0, 0), ("x1", 0, 0, 1)])
            dump("h2T", h2T[:, :, 0:128], [("h2T", 0, 0)])
        if stage("E"):
            break

        for e8 in range(8):
            s_w1, t_w1 = wget((s, "w1", e8))
            s_w2, t_w2 = wget((s, "w2", e8))
            w1v = wview(s_w1, 0, 8, 512)
            w2v = wview(s_w2, 0, 4, 1024)
            for tg in range(4):
                ab = (e8 * 4 + tg) % 2
                h2t = [("h2T", tg, tt) for tt in range(4)]
                for fl in range(4):
                    bk = nbank(B03)
                    rb = fl % 2
                    for kc in range(8):
                        I("pe", "matmul", h2t + [t_w1], [("ps", bk)], ps(bk), lhsT=w1v[:, kc, fl * 128:(fl + 1) * 128],
                          rhs=h2T[:, kc, tg * 512:(tg + 1) * 512], start=(kc == 0), stop=(kc == 7))
                    I("act", "activation", [("ps", bk)], [("rr", rb)], out=rr[:, rb, :], in_=ps(bk), func=AF.Relu)
                    I("pool", "tensor_tensor", [("rr", rb)], [("aT", ab, fl)], out=aT[:, ab, fl, :], in0=rr[:, rb, :], in1=rr[:, rb, :], op=ALU.mult)
                for tt in range(4):
                    for mh in range(2):
                        bk = nbank(B47)
                        for fl in range(4):
                            I("pe", "matmul", [("aT", ab, fl), t_w2], [("ps", bk)], ps(bk), lhsT=aT[:, ab, fl, tt * 128:(tt + 1) * 128],
                              rhs=w2v[:, fl, mh * 512:(mh + 1) * 512], start=(fl == 0), stop=(fl == 3))
                        I("dve", "tensor_tensor", [("ps", bk), ("x1", tg, tt, mh)], [("x1", tg, tt, mh)], out=x1(tg)[:, tt, mh * 512:(mh + 1) * 512],
                          in0=ps(bk), in1=x1(tg)[:, tt, mh * 512:(mh + 1) * 512], op=ALU.add)
            wdone((s, "w1", e8))
            wdone((s, "w2", e8))
        if debug and s == 0:
            dump("x2_0", x1(0)[:, 0, :], [("x1", 0, 0, 0), ("x1", 0, 0, 1)])

        if s + 1 < nseq:
            wget((s + 1, "dv"))
        for i in range(16):
            tg, tt = i // 4, i % 4
            b2 = i % 2
            slot = (32 + i) % 3
            ssa = sm2[:, 16 + b2:17 + b2]
            rsa = sm2[:, 20 + b2:21 + b2]
            tma = sm2[:, 24 + b2:25 + b2]
            x1t = [("x1", tg, tt, 0), ("x1", tg, tt, 1)]
            I("act", "activation", x1t, ["junk2", ("ssG", b2)], out=junk2, in_=x1(tg)[:, tt, :], func=AF.Square, accum_out=ssa)
            rstd_chain(ssa, rsa, 1.0 / D, [("ssG", b2)], ("rsG", b2), tma)
            I("dve", "scalar_tensor_tensor", x1t + [("rsG", b2), "gfin_b"], [("xr", slot)], out=xring[:, slot, :], in0=x1(tg)[:, tt, :], scalar=rsa,
              in1=gfin_b, op0=ALU.mult, op1=ALU.mult)
            I("sp", "dma_start", [("xr", slot)], [("out", s, i)], out=out_d[row0 + i * 128:row0 + (i + 1) * 128, :], in_=xring[:, slot, :], dma=True)

    es = ExitStack()
    with es:
        es.enter_context(nc.allow_non_contiguous_dma(reason="tiny strided parameter loads"))
        P.emit(nc, es)
    return nc, dbg_map


def _rel_bucket_np(rel):
    nb = 16
    ret = np.where(rel > 0, nb, 0)
    n = np.abs(rel)
    max_exact = nb // 2
    nf = np.maximum(n, 1).astype(np.float32)
    large = max_exact + (np.log(nf / np.float32(max_exact)) / np.float32(math.log(128 / max_exact))
                         * np.float32(nb - max_exact)).astype(np.int32)
    large = np.minimum(large, nb - 1)
    return ret + np.where(n < max_exact, n, large)


def _constants():
    k = np.arange(128)[:, None]
    q = np.arange(128)[None, :]
    cst = np.zeros((128, 6, 128), np.float32)
    cst[:, 0, :] = np.eye(128, dtype=np.float32)
    cst[:, 1, :] = (k <= q).astype(np.float32)
    cst[:, 2, :] = 1.0
    cst[:, 3, :] = np.where(k <= q, 0.0, NEG)
    seg = np.ones((8, 16), np.float32)
    seg[:, 0] = 0.0
    cst[:, 4, :] = seg.reshape(1, 128)
    cst[:, 5, :] = np.where((k // 64) <= (q // 64), 0.0, NEG)
    idx = np.zeros((128, 2, 128), np.int64)
    idx[:, 0, :] = _rel_bucket_np((k - q).astype(np.int32))
    idx[:, 1, :] = _rel_bucket_np((k - q - 128).astype(np.int32))
    return cst.reshape(128, 6 * 128), idx


_CACHE = {}


def kernel(**inputs):
    x = np.ascontiguousarray(np.asarray(inputs["x"], np.float32))
    cst, idx = _constants()
    rel = np.asarray(inputs["rel_table"], np.float32)
    braw = np.ascontiguousarray(np.transpose(rel[idx], (0, 3, 1, 2))).reshape(128, 4 * 2 * 128)
    lamv = np.concatenate([np.asarray(inputs[k], np.float32).reshape(-1) for k in ("lam_q1", "lam_k1", "lam_q2", "lam_k2")])
    shared = {
        "w_in": np.ascontiguousarray(np.asarray(inputs["w_in"], np.float32)[0]),
        "w_pa": np.ascontiguousarray(np.asarray(inputs["w_pa"], np.float32)[0]),
        "w_pb": np.ascontiguousarray(np.asarray(inputs["w_pb"], np.float32)[0]),
        "w_o": np.ascontiguousarray(np.asarray(inputs["w_o"], np.float32)[0]),
        "w_1": np.ascontiguousarray(np.asarray(inputs["w_1"], np.float32)[0]),
        "w_2": np.ascontiguousarray(np.asarray(inputs["w_2"], np.float32)[0]),
        "g_mix": np.asarray(inputs["g_mix"], np.float32).reshape(-1),
        "g_mlp": np.asarray(inputs["g_mlp"], np.float32).reshape(-1),
        "g_final": np.asarray(inputs["g_final"], np.float32).reshape(-1),
        "g_subln": np.asarray(inputs["g_subln"], np.float32).reshape(-1),
        "b_f": np.asarray(inputs["b_f"], np.float32).reshape(-1),
        "lamv": lamv,
        "rel_table": rel,
        "cst": cst,
        "bias_raw": braw,
    }
    if "nc" not in _CACHE:
        _CACHE["nc"] = build()[0]
    nc = _CACHE["nc"]
    in_maps = []
    for c in range(NCORES):
        m = dict(shared)
        m["x"] = x[NSEQ * c:NSEQ * (c + 1)].reshape(NSEQ * SEQ, D)
        in_maps.append(m)
    res = run_bass_kernel_spmd(nc, in_maps, core_ids=list(range(NCORES)))
    out = np.concatenate([np.asarray(r["out"]).reshape(NSEQ, SEQ, D) for r in res.results], axis=0)
    return out.astype(np.float32)
```

```python
import math
from contextlib import ExitStack

import numpy as np
import concourse.bass as bass
import concourse.mybir as mybir
from concourse.bass_utils import run_bass_kernel_spmd

F32 = mybir.dt.float32
BF16 = mybir.dt.bfloat16
AF = mybir.ActivationFunctionType
ALU = mybir.AluOpType

NCORES = 8
SEQ = 2048
D = 1024
NSEQ = 2
NEG = -30000.0
EPS = 1e-6
NR = 5
COL_DQ, COL_DK, COL_DV, COL_FQ, COL_FK, COL_FV, COL_FL, COL_GL = 0, 512, 1024, 1536, 2048, 2560, 3072, 3080


class _Op:
    __slots__ = ("eng", "fn", "deps", "is_dma", "signal", "seq", "sem_key", "target", "prev_target", "idx")


class Prog:
    ENGS = ("pe", "act", "dve", "pool", "sp")
    DMA_POOLS = {"sp": 16, "pool": 16, "act": 4}

    def __init__(self):
        self.ops = []
        self.last_w = {}
        self.readers = {}
        self.pending_barrier = {}
        self.last_on = {}
        self.dma_rr = {q: 0 for q in self.DMA_POOLS}
        self.dma_tot = {}

    def ins(self, eng, method, reads, writes, *args, dma=False, **kwargs):
        return self.op(eng, (method, args, kwargs), reads, writes, dma)

    def op(self, eng, fn, reads=(), writes=(), dma=False):
        o = _Op()
        o.eng, o.fn, o.is_dma, o.signal, o.seq, o.idx = eng, fn, dma, False, 0, len(self.ops)
        deps = {}

        def add(d):
            if d is not None:
                deps[d.idx] = d

        for t in reads:
            add(self.last_w.get(t))
        for t in writes:
            add(self.last_w.get(t))
            r = self.readers.get(t)
            if r:
                for k, v in r.items():
                    if k == "dma":
                        for d in v:
                            add(d)
                    else:
                        add(v)
        pb = self.pending_barrier.pop(eng, None)
        if pb:
            for d in pb:
                add(d)
        dl = []
        for d in deps.values():
            if d.eng == "pe" and eng == "pe" and not d.is_dma and not dma:
                continue
            dl.append(d)
            if not d.is_dma:
                d.signal = True
        o.deps = dl
        if dma:
            n = self.DMA_POOLS[eng]
            i = self.dma_rr[eng] % n
            self.dma_rr[eng] += 1
            key = (eng, i)
            tot = self.dma_tot.get(key, 0)
            o.sem_key, o.prev_target, o.target = key, tot, tot + 16
            self.dma_tot[key] = tot + 16
        for t in writes:
            self.last_w[t] = o
            self.readers[t] = {}
        for t in reads:
            r = self.readers.setdefault(t, {})
            if dma:
                r.setdefault("dma", []).append(o)
            else:
                r[eng] = o
        self.ops.append(o)
        self.last_on[eng] = o
        return o

    def barrier(self):
        lasts = list(self.last_on.values())
        for e in self.ENGS:
            self.pending_barrier[e] = list(lasts)

    def emit(self, nc, es):
        sems = {e: es.enter_context(nc.semaphore("s_" + e)) for e in self.ENGS}
        dsem = {}
        for q, n in self.DMA_POOLS.items():
            for i in range(n):
                dsem[(q, i)] = es.enter_context(nc.semaphore("d_%s%d" % (q, i)))
        cnt = {e: 0 for e in self.ENGS}
        for o in self.ops:
            if not o.is_dma and o.signal:
                cnt[o.eng] += 1
                o.seq = cnt[o.eng]
        by_eng = {e: [o for o in self.ops if o.eng == e] for e in self.ENGS}
        dma_tot = self.dma_tot

        def body(e, E):
            known = {}

            def wait(key, val):
                if val > 0 and known.get(key, 0) < val:
                    E.wait_ge(sems[key] if isinstance(key, str) else dsem[key], val)
                    known[key] = val

            for o in by_eng[e]:
                for d in o.deps:
                    if d.is_dma:
                        wait(d.sem_key, d.target)
                    else:
                        wait(d.eng, d.seq)
                meth, a, kw = o.fn
                if o.is_dma:
                    wait(o.sem_key, o.prev_target)
                    getattr(E, meth)(*a, **kw).then_inc(dsem[o.sem_key], 16)
                else:
                    ins = getattr(E, meth)(*a, **kw)
                    if o.signal:
                        ins.then_inc(sems[e], 1)
            for (q, i), tot in dma_tot.items():
                if q == e:
                    wait((q, i), tot)

        block = es.enter_context(nc.Block())

        @block.tensor
        def _(E):
            body("pe", E)

        @block.scalar
        def _(E):
            body("act", E)

        @block.vector
        def _(E):
            body("dve", E)

        @block.gpsimd
        def _(E):
            body("pool", E)

        @block.sync
        def _(E):
            body("sp", E)


def build(nseq=NSEQ, stop_after=None, debug=False):
    nc = bass.Bass("TRN2", target_bir_lowering=False)
    P = Prog()
    I = P.ins

    def din(name, shape, dt=F32):
        return nc.dram_tensor(name, list(shape), dt, kind="ExternalInput").ap()

    x_d = din("x", [NSEQ * SEQ, D])
    w_in = din("w_in", [D, 5128])
    w_pa = din("w_pa", [512, D])
    w_pb = din("w_pb", [512, D])
    w_o = din("w_o", [D, D])
    w_1 = din("w_1", [D, 4096])
    w_2 = din("w_2", [4096, D])
    g_mix = din("g_mix", [D])
    g_mlp = din("g_mlp", [D])
    g_final = din("g_final", [D])
    g_subln = din("g_subln", [128])
    b_f = din("b_f", [8])
    lamv = din("lamv", [4 * 64])
    rel_t = din("rel_table", [32, 4])
    cst_d = din("cst", [128, 6 * 128])
    braw_d = din("bias_raw", [128, 4 * 2 * 128])
    out_d = nc.dram_tensor("out", [NSEQ * SEQ, D], F32, kind="ExternalOutput").ap()
    dbg_d = None
    dbg_map = {}
    dbg_off = [0]
    if debug:
        dbg_d = nc.dram_tensor("dbg", [128, 65536], F32, kind="ExternalOutput").ap()

    def sb(name, shape, dt):
        return nc.alloc_sbuf_tensor(name, list(shape), dt).ap()

    cstf = sb("cstf", [128, 6, 128], F32)
    ident_bf = sb("ident_bf", [128, 128], BF16)
    maskf_bf = sb("maskf_bf", [128, 128], BF16)
    bd = sb("bd", [128, 4, 2, 2, 128], BF16)
    gmix_c = sb("gmix_c", [128, 8], F32)
    gmlp_c = sb("gmlp_c", [128, 8], F32)
    gfin_b = sb("gfin_b", [128, 1024], F32)
    gs_b = sb("gs_b", [128, 128], F32)
    bf_b = sb("bf_b", [128, 8], F32)
    lam_b = sb("lam_b", [128, 4, 64], F32)
    t15 = sb("t15", [128, 4], F32)
    misc = sb("misc", [128, 32], F32)
    wfl = sb("wfl", [128, 8, 8], BF16)
    ring = sb("ring", [128, NR, 4096], BF16)
    xring = sb("xring", [128, 3, 1024], F32)
    HO = sb("HO", [128, 4, 4096], F32)
    OVB = 78464
    OV = sb("OV", [128, OVB // 4], F32)
    psum = nc.alloc_psum_tensor("psum", [128, 8, 512], F32).ap()

    def hT(tg):
        return HO[:, tg, 0:2048].bitcast(BF16).rearrange("p (k n) -> p k n", n=512)

    def oT(tg):
        return HO[:, tg, 2048:4096].bitcast(BF16).rearrange("p (k n) -> p k n", n=512)

    def x1(tg):
        return HO[:, tg, :].rearrange("p (t n) -> p t n", n=1024)

    class Carver:
        def __init__(self):
            self.off = 0

        def take(self, nbytes):
            o = self.off
            self.off += (nbytes + 31) // 32 * 32
            assert self.off <= OVB, self.off
            return o

        def f32(self, n):
            o = self.take(n * 4)
            return OV[:, o // 4:o // 4 + n]

        def bf16(self, n):
            o = self.take(n * 2)
            return OV[:, o // 4:o // 4 + (n + 1) // 2].bitcast(BF16)[:, 0:n]

    c1 = Carver()
    Vd = c1.bf16(16 * 4 * 129).rearrange("p (j h e) -> p j h e", j=16, h=4)
    Vf = c1.bf16(16 * 8 * 65).rearrange("p (j h e) -> p j h e", j=16, h=8)
    qT = c1.bf16(2 * 2048).rearrange("p (m n) -> p m n", m=2)
    kTz = c1.bf16(2 * 2048).rearrange("p (m n) -> p m n", m=2)
    ET = c1.bf16(3 * 2 * 512).rearrange("p (b m n) -> p b m n", b=3, m=2)
    xn = c1.bf16(2 * 1024).rearrange("p (b n) -> p b n", b=2)
    junk = c1.bf16(1024)
    fz = c1.f32(5 * 128).rearrange("p (a n) -> p a n", a=5)
    Csb = c1.f32(128).rearrange("p (h j) -> p h j", h=8)
    Rsb = c1.f32(128).rearrange("p (h j) -> p h j", h=8)
    ft0 = c1.f32(2 * 128).rearrange("p (b n) -> p b n", b=2)
    ftt = c1.f32(4 * 128).rearrange("p (b n) -> p b n", b=4)
    obf = c1.bf16(4 * 128).rearrange("p (b n) -> p b n", b=4)
    sm1 = c1.f32(64)
    c0 = Carver()
    braw = c0.f32(4 * 2 * 128).rearrange("p (h t n) -> p h t n", h=4, t=2)
    bfull = c0.f32(128)
    c2 = Carver()
    h2T = c2.bf16(8 * 2048).rearrange("p (k n) -> p k n", k=8)
    aT = c2.bf16(2 * 4 * 512).rearrange("p (b k n) -> p b k n", b=2, k=4)
    mT = c2.bf16(8 * 512).rearrange("p (k n) -> p k n", k=8)
    sg = c2.f32(2 * 2 * 512).rearrange("p (a b n) -> p a b n", a=2, b=2)
    m01 = c2.f32(2 * 2 * 512).rearrange("p (a b n) -> p a b n", a=2, b=2)
    rr = c2.f32(2 * 512).rearrange("p (b n) -> p b n", b=2)
    xn2 = c2.bf16(2 * 1024).rearrange("p (b n) -> p b n", b=2)
    junk2 = c2.bf16(1024)
    sm2 = c2.f32(64)

    def ps(b):
        return psum[:, b, :]

    def ps_bf(b):
        return psum[:, b, :].bitcast(BF16)

    def dump(name, ap, reads):
        if not debug:
            return
        shp = list(ap.shape)
        n = 1
        for s_ in shp[1:]:
            n *= s_
        o = dbg_off[0]
        dbg_off[0] += n
        assert dbg_off[0] <= 65536
        dbg_map[name] = (o, shp)
        dst = dbg_d[0:shp[0], o:o + n]
        if len(shp) == 3:
            dst = dst.rearrange("p (a b) -> p a b", a=shp[1])
        elif len(shp) == 4:
            dst = dst.rearrange("p (a b c) -> p a b c", a=shp[1], b=shp[2])
        elif len(shp) == 5:
            dst = dst.rearrange("p (a b c d) -> p a b c d", a=shp[1], b=shp[2], c=shp[3])
        I("pool", "dma_start", reads, [("dbg", name)], out=dst, in_=ap, dma=True)

    def win_cols(c0, n):
        return w_in[:, c0:c0 + n].rearrange("(k p) n -> p k n", p=128)

    def kp(ap):
        return ap.rearrange("(k p) n -> p k n", p=128)

    def unit_cols(u):
        if u < 4:
            return COL_DQ + u * 128, COL_DK + u * 128
        return COL_FQ + (u - 4) * 128, COL_FK + (u - 4) * 128

    sched = []
    for s in range(nseq):
        sched.append(((s, "dv"), [(0, 8, 512, win_cols(COL_DV, 512))]))
        sched.append(((s, "fv"), [(0, 8, 512, win_cols(COL_FV, 512))]))
        for u in range(8):
            cq, ck = unit_cols(u)
            sched.append(((s, "u", u), [(0, 8, 128, win_cols(cq, 128)), (1024, 8, 128, win_cols(ck, 128))]))
        for tg in range(4):
            for nm in ("g0", "g2", "pl", "g1", "g3", "ph", "wo0", "wo1"):
                if nm[0] == "g":
                    parts = [(0, 8, 512, win_cols(COL_GL + 512 * int(nm[1]), 512))]
                elif nm == "pl":
                    parts = [(0, 4, 512, kp(w_pa[:, 0:512])), (2048, 4, 512, kp(w_pb[:, 0:512]))]
                elif nm == "ph":
                    parts = [(0, 4, 512, kp(w_pa[:, 512:1024])), (2048, 4, 512, kp(w_pb[:, 512:1024]))]
                else:
                    mh = int(nm[2])
                    parts = [(0, 8, 512, kp(w_o[:, mh * 512:(mh + 1) * 512]))]
                sched.append(((s, tg, nm), parts))
        for e8 in range(8):
            sched.append(((s, "w1", e8), [(0, 8, 512, kp(w_1[:, e8 * 512:(e8 + 1) * 512]))]))
            sched.append(((s, "w2", e8), [(0, 4, 1024, kp(w_2[e8 * 512:(e8 + 1) * 512, :]))]))
    sched_idx = {k: i for i, (k, _) in enumerate(sched)}
    wstate = {"issued": 0, "want": 0}
    released = set()
    LOOKAHEAD = NR - 1

    def wpump():
        while wstate["issued"] < min(wstate["want"], len(sched)):
            i = wstate["issued"]
            if i >= NR and (i - NR) not in released:
                break
            slot = i % NR
            for (doff, k, n, src) in sched[i][1]:
                dst = ring[:, slot, doff:doff + k * n].rearrange("p (k n) -> p k n", k=k)
                I("pool", "dma_start", [], [("ring", slot)], out=dst, in_=src, dma=True)
            wstate["issued"] += 1

    def wget(key):
        idx = sched_idx[key]
        wstate["want"] = max(wstate["want"], idx + 1 + LOOKAHEAD)
        wpump()
        assert wstate["issued"] > idx, key
        return idx % NR, ("ring", idx % NR)

    def wdone(key):
        released.add(sched_idx[key])
        wpump()

    def wview(slot, doff, k, n):
        return ring[:, slot, doff:doff + k * n].rearrange("p (k n) -> p k n", k=k)

    I("sp", "dma_start", [], ["cstf"], out=cstf.rearrange("p a n -> p (a n)"), in_=cst_d, dma=True)
    I("sp", "dma_start", [], ["braw"], out=braw.rearrange("p h t n -> p (h t n)"), in_=braw_d, dma=True)
    I("sp", "dma_start", [], ["gmix_c"], out=gmix_c, in_=g_mix.rearrange("(c p) -> p c", p=128), dma=True)
    I("sp", "dma_start", [], ["gmlp_c"], out=gmlp_c, in_=g_mlp.rearrange("(c p) -> p c", p=128), dma=True)
    I("sp", "dma_start", [], ["gfin_b"], out=gfin_b, in_=g_final.partition_broadcast(128), dma=True)
    I("sp", "dma_start", [], ["gs_b"], out=gs_b, in_=g_subln.partition_broadcast(128), dma=True)
    I("sp", "dma_start", [], ["bf_b"], out=bf_b, in_=b_f.partition_broadcast(128), dma=True)
    I("sp", "dma_start", [], ["lam_b"], out=lam_b.rearrange("p a n -> p (a n)"), in_=lamv.partition_broadcast(128), dma=True)
    I("sp", "dma_start", [], ["t15"], out=t15, in_=rel_t[15, :].partition_broadcast(128), dma=True)
    I("pool", "dma_start", [], ["wfl"], out=wfl, in_=win_cols(COL_FL, 8), dma=True)

    I("pool", "memset", [], ["misc0"], misc[:, 0:1], EPS)
    I("pool", "memset", [], ["misc6"], misc[:, 6:7], 1.0)
    I("dve", "tensor_copy", ["cstf"], ["ident_bf"], out=ident_bf, in_=cstf[:, 0, :])
    I("dve", "tensor_copy", ["cstf"], ["maskf_bf"], out=maskf_bf, in_=cstf[:, 3, :])
    I("dve", "tensor_scalar", ["gs_b"], ["gs_b"], out=gs_b, in0=gs_b, scalar1=0.8, scalar2=None, op0=ALU.mult)
    I("dve", "tensor_tensor", ["lam_b"], ["bfull"], out=bfull[:, 0:64], in0=lam_b[:, 0, :], in1=lam_b[:, 1, :], op=ALU.mult)
    I("dve", "tensor_scalar", ["bfull"], ["bfull", "misc1"], out=bfull[:, 0:64], in0=bfull[:, 0:64], scalar1=1.0, scalar2=None,
      op0=ALU.mult, op1=ALU.add, accum_out=misc[:, 1:2])
    I("dve", "tensor_tensor", ["lam_b", "bfull"], ["bfull"], out=bfull[:, 64:128], in0=lam_b[:, 2, :], in1=lam_b[:, 3, :], op=ALU.mult)
    I("dve", "tensor_scalar", ["bfull"], ["bfull", "misc2"], out=bfull[:, 64:128], in0=bfull[:, 64:128], scalar1=1.0, scalar2=None,
      op0=ALU.mult, op1=ALU.add, accum_out=misc[:, 2:3])
    I("act", "activation", ["misc1", "misc2"], ["misc34"], out=misc[:, 3:5], in_=misc[:, 1:3], func=AF.Exp)
    I("dve", "tensor_tensor", ["misc34"], ["misc5"], out=misc[:, 5:6], in0=misc[:, 3:4], in1=misc[:, 4:5], op=ALU.subtract)
    I("dve", "tensor_scalar", ["misc5"], ["misc5"], out=misc[:, 5:6], in0=misc[:, 5:6], scalar1=0.2, scalar2=-1.0, op0=ALU.add, op1=ALU.mult)
    for h in range(4):
        for ty in range(2):
            if ty == 0:
                I("dve", "scalar_tensor_tensor", ["braw", "t15", "cstf", "bd"], ["bfull"], out=bfull, in0=braw[:, h, ty, :], scalar=t15[:, h:h + 1],
                  in1=cstf[:, 5, :], op0=ALU.subtract, op1=ALU.add)
            else:
                I("dve", "tensor_scalar", ["braw", "t15", "bd"], ["bfull"], out=bfull, in0=braw[:, h, ty, :], scalar1=t15[:, h:h + 1],
                  scalar2=None, op0=ALU.subtract)
            I("dve", "tensor_copy", ["bfull"], ["bd_hi"], out=bd[:, h, ty, 0, :], in_=bfull)
            I("dve", "tensor_tensor", ["bfull", "bd_hi"], ["bd"], out=bd[:, h, ty, 1, :], in0=bfull, in1=bd[:, h, ty, 0, :], op=ALU.subtract)
    dump("bd", bd, ["bd"])
    dump("misc", misc, ["misc5"])

    mm_rot = {"i": 0}
    B03 = [0, 1, 2, 3]
    B47 = [4, 5, 6, 7]
    B07 = list(range(8))

    def nbank(allowed):
        b = allowed[mm_rot["i"] % len(allowed)]
        mm_rot["i"] += 1
        return b

    def rstd_chain(ss_ap, out_ap, scale, toks_in, tok_out, tmp_ap):
        I("act", "activation", list(toks_in) + ["misc0"], [("tmp", tok_out)], out=tmp_ap, in_=ss_ap, func=AF.Ln, bias=misc[:, 0:1], scale=scale)
        I("act", "activation", [("tmp", tok_out)], [tok_out], out=out_ap, in_=tmp_ap, func=AF.Exp, scale=-0.5)

    done = {"stop": False}

    def stage(name):
        if stop_after == name:
            done["stop"] = True
        return done["stop"]

    def hT_toks(tg):
        return [("hT", tg, j) for j in range(4)]

    def oT_toks(tg):
        return [("oT", tg, u, qq) for u in range(8) for qq in range(4)]

    fin_q = []
    fin_state = {"n": 0}

    def fin_step(flush=False):
        while True:
            for gen in list(fin_q):
                try:
                    next(gen)
                except StopIteration:
                    fin_q.remove(gen)
            if not flush or not fin_q:
                break

    def attention_group(s, u, ub, g, et_rr):
        is_diff = u < 4
        width = 129 if is_diff else 65
        hd = 128 if is_diff else 64
        nkb = 4 * g + 4
        sbuf_i = {"n": 0}
        pend = []

        def acc_ap(a):
            bank = 4 + a // 2
            c = (a % 2) * 129
            return psum[:, bank, c:c + width], bank

        def do_qk(kb):
            j = kb - 4 * g
            c0 = max(0, j) * 128
            pair = sbuf_i["n"] % 2
            sbuf_i["n"] += 1
            eb = et_rr["i"] % 3
            et_rr["i"] += 1
            tgk = kb // 4
            for m in range(2):
                bk = pair * 2 + m
                extra = []
                if is_diff:
                    for qq in range(max(0, j), 4):
                        Q = 4 * g + qq
                        if Q == kb:
                            extra.append((qq, 0))
                        elif Q == kb + 1:
                            extra.append((qq, 1))
                elif j >= 0:
                    extra.append((j, -1))
                nex = len(extra) * (2 if is_diff else 1)
                I("pe", "matmul", [("kT", tgk, m), "kTz0", ("qT", g, m), "qT0"], [("ps", bk)],
                  ps(bk)[:, c0:512], lhsT=kTz[:, m, kb * 128:(kb + 1) * 128], rhs=qT[:, m, g * 512 + c0:(g + 1) * 512],
                  start=True, stop=(nex == 0))
                k_ex = 0
                for (qq, ty) in extra:
                    if is_diff:
                        for hl in range(2):
                            k_ex += 1
                            I("pe", "matmul", ["ident_bf", "bd"], [("ps", bk)], ps(bk)[:, qq * 128:(qq + 1) * 128], lhsT=ident_bf,
                              rhs=bd[:, u, ty, hl, :], start=False, stop=(k_ex == nex))
                    else:
                        k_ex += 1
                        I("pe", "matmul", ["ident_bf", "maskf_bf"], [("ps", bk)], ps(bk)[:, qq * 128:(qq + 1) * 128], lhsT=ident_bf,
                          rhs=maskf_bf, start=False, stop=(k_ex == nex))
            if is_diff:
                I("act", "activation", [("ps", pair * 2), ("ps", pair * 2 + 1)], [("ET", eb, 0), ("ET", eb, 1)],
                  out=ET[:, eb, :, c0:512], in_=psum[:, pair * 2:pair * 2 + 2, c0:512], func=AF.Exp)
            else:
                for m in range(2):
                    hh = 2 * (u - 4) + m
                    I("act", "activation", [("ps", pair * 2 + m), "Csb"], [("ET", eb, m)],
                      out=ET[:, eb, m, c0:512], in_=psum[:, pair * 2 + m, c0:512], func=AF.Exp, bias=Csb[:, hh, kb:kb + 1], scale=1.0)
            pend.append((kb, eb))

        def finalize(qq):
            slot = fin_state["n"] % 4
            fin_state["n"] += 1
            fb = slot % 2
            a0, a1 = qq * 2, qq * 2 + 1
            p0, _ = acc_ap(a0)
            p1, _ = acc_ap(a1)
            base = 16 + 8 * slot
            rc = sm1[:, base:base + 2]
            nl = sm1[:, base + 2:base + 3]
            ssq = sm1[:, base + 3:base + 4]
            rsd = sm1[:, base + 4:base + 5]
            tmp = sm1[:, base + 5:base + 6]
            I("dve", "reciprocal", [("accb", qq)], [("rc", slot, 0)], out=rc[:, 0:1], in_=p0[:, hd:hd + 1])
            I("dve", "reciprocal", [("accb", qq)], [("rc", slot, 1)], out=rc[:, 1:2], in_=p1[:, hd:hd + 1])
            if is_diff:
                I("dve", "tensor_tensor", [("rc", slot, 1), "misc5"], [("nl", slot)], out=nl, in0=rc[:, 1:2], in1=misc[:, 5:6], op=ALU.mult)
                I("dve", "tensor_scalar", [("accb", qq), ("rc", slot, 0)], [("ft0", fb)], out=ft0[:, fb, :], in0=p0[:, 0:128], scalar1=rc[:, 0:1],
                  scalar2=None, op0=ALU.mult)
                I("dve", "scalar_tensor_tensor", [("accb", qq), ("nl", slot), ("ft0", fb)], [("ftt", slot)], out=ftt[:, slot, :], in0=p1[:, 0:128],
                  scalar=nl, in1=ft0[:, fb, :], op0=ALU.mult, op1=ALU.add)
                I("dve", "tensor_tensor", [("ftt", slot), ("ft0", fb)], [("ft0", fb)], out=ft0[:, fb, :], in0=ftt[:, slot, :], in1=ftt[:, slot, :],
                  op=ALU.mult)
                I("dve", "tensor_scalar", [("ft0", fb)], [("ft0", fb), ("ssq", slot)], out=ft0[:, fb, :], in0=ft0[:, fb, :], scalar1=1.0, scalar2=None,
                  op0=ALU.mult, op1=ALU.add, accum_out=ssq)
                yield
                rstd_chain(ssq, rsd, 1.0 / 128, [("ssq", slot)], ("rsd", slot), tmp)
                yield
                I("dve", "scalar_tensor_tensor", [("ftt", slot), ("rsd", slot), "gs_b"], [("obf", slot)], out=obf[:, slot, :], in0=ftt[:, slot, :],
                  scalar=rsd, in1=gs_b, op0=ALU.mult, op1=ALU.mult)
            else:
                I("dve", "tensor_scalar", [("accb", qq), ("rc", slot, 0)], [("obf", slot)], out=obf[:, slot, 0:64], in0=p0[:, 0:64], scalar1=rc[:, 0:1],
                  scalar2=None, op0=ALU.mult)
                I("dve", "tensor_scalar", [("accb", qq), ("rc", slot, 1), ("obf", slot)], [("obf", slot)], out=obf[:, slot, 64:128], in0=p1[:, 0:64],
                  scalar1=rc[:, 1:2], scalar2=None, op0=ALU.mult)
            yield
            I("pe", "transpose", [("obf", slot), "ident_bf"], [("accb", qq)], out=ps_bf(4 + qq)[:, 0:128], in_=obf[:, slot, :], identity=ident_bf)
            yield
            I("dve", "tensor_copy", [("accb", qq)], [("oT", g, u, qq)], out=oT(g)[:, u, qq * 128:(qq + 1) * 128], in_=ps_bf(4 + qq)[:, 0:128])

        def do_pv(kb, eb):
            j = kb - 4 * g
            seen = set()
            for qq in range(max(0, j), 4):
                for m in range(2):
                    a = qq * 2 + m
                    oap, bank = acc_ap(a)
                    if is_diff:
                        rhs = Vd[:, kb, u, :]
                        vt = [("Vd", kb), "Vd_ones"]
                    else:
                        rhs = Vf[:, kb, 2 * (u - 4) + m, :]
                        vt = [("Vf", kb), "Vf_ones"]
                    st = (kb == 0 and bank not in seen)
                    seen.add(bank)
                    I("pe", "matmul", [("ET", eb, m)] + vt, [("accb", qq)], oap, lhsT=ET[:, eb, m, qq * 128:(qq + 1) * 128], rhs=rhs,
                      start=st, stop=(kb == 4 * g + qq), skip_group_check=True)
            if j >= 0:
                gen = finalize(j)
                next(gen)
                fin_q.append(gen)

        do_qk(0)
        for kb in range(nkb):
            if kb + 1 < nkb:
                do_qk(kb + 1)
            kbp, ebp = pend.pop(0)
            if kbp == 0:
                fin_step(flush=True)
            else:
                fin_step()
            do_pv(kbp, ebp)

    for s in range(nseq):
        if done["stop"]:
            break
        row0 = s * SEQ
        s_dv, t_dv = wget((s, "dv"))
        P.barrier()
        I("pool", "memset", [], ["Vd_ones"], Vd[:, :, :, 128:129], 1.0)
        I("pool", "memset", [], ["Vf_ones"], Vf[:, :, :, 64:65], 1.0)
        I("pool", "memset", [], ["kTz0"], kTz[64:128, 0, :], 0.0)
        I("pool", "memset", [], ["kTz0"], kTz[0:64, 1, :], 0.0)
        I("pool", "memset", [], ["qT0"], qT[64:128, 0, :], 0.0)
        I("pool", "memset", [], ["qT0"], qT[0:64, 1, :], 0.0)

        def xload(i):
            slot = i % 3
            I("sp", "dma_start", [], [("xr", slot)], out=xring[:, slot, :], in_=x_d[row0 + i * 128:row0 + (i + 1) * 128, :], dma=True)

        vw = {}
        for wkey in ("dv", "fv"):
            slot_w, tok_w = wget((s, wkey))
            vw[wkey] = (wview(slot_w, 0, 8, 512), tok_w)

        def vproj_tile(i):
            for (wkey, dst, nh, hd, vtok) in (("dv", Vd, 4, 128, "Vd"), ("fv", Vf, 8, 64, "Vf")):
                wv, tok_w = vw[wkey]
                tg, off = i // 4, (i % 4) * 128
                bk = nbank(B03)
                for kc in range(8):
                    I("pe", "matmul", [("hT", tg, i % 4), tok_w], [("ps", bk)], ps(bk), lhsT=hT(tg)[:, kc, off:off + 128], rhs=wv[:, kc, :],
                      start=(kc == 0), stop=(kc == 7))
                src_ = ps(bk).rearrange("p (h e) -> p h e", h=nh)
                if wkey == "dv":
                    I("act", "activation", [("ps", bk)], [(vtok, i)], out=dst[:, i, :, 0:hd], in_=src_, func=AF.Copy)
                else:
                    I("dve", "tensor_copy", [("ps", bk)], [(vtok, i)], out=dst[:, i, :, 0:hd], in_=src_)

        xload(0)
        xload(1)
        for i in range(16):
            if i + 2 < 16:
                xload(i + 2)
            if i >= 1:
                vproj_tile(i - 1)
            slot = i % 3
            b2 = i % 2
            tg, off = i // 4, (i % 4) * 128
            ssa = sm1[:, b2:b2 + 1]
            rsa = sm1[:, 4 + b2:5 + b2]
            tma = sm1[:, 8 + b2:9 + b2]
            I("act", "activation", [("xr", slot)], ["junk", ("ssA", b2)], out=junk, in_=xring[:, slot, :], func=AF.Square, accum_out=ssa)
            rstd_chain(ssa, rsa, 1.0 / D, [("ssA", b2)], ("rsA", b2), tma)
            I("pool", "tensor_scalar", [("xr", slot), ("rsA", b2)], [("xn", b2)], out=xn[:, b2, :], in0=xring[:, slot, :], scalar1=rsa, scalar2=1.0,
              op0=ALU.mult, op1=ALU.mult)
            bk = nbank(B03)
            for c in range(8):
                I("pe", "transpose", [("xn", b2), "ident_bf"], [("ps", bk)], out=ps_bf(bk)[:, c * 128:(c + 1) * 128],
                  in_=xn[:, b2, c * 128:(c + 1) * 128], identity=ident_bf)
            I("dve", "tensor_tensor", [("ps", bk), "gmix_c"], [("hT", tg, i % 4)], out=hT(tg)[:, :, off:off + 128],
              in0=ps_bf(bk).rearrange("p (c n) -> p c n", c=8), in1=gmix_c.unsqueeze(2).to_broadcast([128, 8, 128]), op=ALU.mult)
        if debug and s == 0:
            for tg in range(4):
                dump("hT%d" % tg, hT(tg), hT_toks(tg))
        if stage("A"):
            break

        vproj_tile(15)
        wdone((s, "dv"))
        wdone((s, "fv"))
        wget((s, "u", 0))
        bkf = nbank(B03)
        for i in range(16):
            tg, off = i // 4, (i % 4) * 128
            for kc in range(8):
                I("pe", "matmul", [("hT", tg, i % 4), "wfl"], [("ps", bkf)], ps(bkf)[:, i * 8:(i + 1) * 8], lhsT=hT(tg)[:, kc, off:off + 128],
                  rhs=wfl[:, kc, :], start=(kc == 0), stop=(kc == 7), skip_group_check=True)
        zt, Lt, incl, excl = fz[:, 0, :], fz[:, 1, :], fz[:, 2, :], fz[:, 3, :]
        I("dve", "tensor_tensor", [("ps", bkf), "bf_b"], ["fz_z"], out=zt.rearrange("p (j h) -> p j h", h=8),
          in0=ps(bkf)[:, 0:128].rearrange("p (j h) -> p j h", h=8), in1=bf_b.unsqueeze(1).to_broadcast([128, 16, 8]), op=ALU.add)
        I("act", "activation", ["fz_z"], ["fz_e"], out=fz[:, 4, :], in_=zt, func=AF.Exp, scale=-1.0)
        I("act", "activation", ["fz_e", "misc6"], ["fz_L"], out=Lt.rearrange("p (h j) -> p j h", h=8),
          in_=fz[:, 4, :].rearrange("p (j h) -> p j h", h=8), func=AF.Ln, bias=misc[:, 6:7], scale=1.0)
        I("dve", "tensor_tensor_scan", ["fz_L", "cstf"], ["fz_incl"], out=incl, data0=cstf[:, 4, :], data1=Lt, initial=0.0, op0=ALU.mult, op1=ALU.add)
        I("dve", "tensor_tensor", ["fz_incl", "fz_L"], ["fz_excl"], out=excl, in0=incl, in1=Lt, op=ALU.subtract)
        bkc = nbank(B03)
        I("pe", "matmul", ["fz_L", "cstf"], [("ps", bkc)], ps(bkc)[:, 0:128], lhsT=cstf[:, 1, :], rhs=Lt, start=True, stop=False)
        I("pe", "matmul", ["fz_excl", "cstf"], [("ps", bkc)], ps(bkc)[:, 0:128], lhsT=cstf[:, 2, :], rhs=excl, start=False, stop=True)
        bkr = nbank(B03)
        I("pe", "matmul", ["fz_incl", "cstf"], [("ps", bkr)], ps(bkr)[:, 0:128], lhsT=cstf[:, 2, :], rhs=incl, start=True, stop=True)
        I("act", "activation", [("ps", bkc)], ["Csb"], out=Csb.rearrange("p h j -> p (h j)"), in_=ps(bkc)[:, 0:128], func=AF.Copy)
        I("dve", "tensor_copy", [("ps", bkr)], ["Rsb"], out=Rsb.rearrange("p h j -> p (h j)"), in_=ps(bkr)[:, 0:128])
        if debug and s == 0:
            dump("Vd", Vd[:, 0:2], [("Vd", 0), ("Vd", 1), "Vd_ones"])
            dump("Vf", Vf[:, 0:2], [("Vf", 0), ("Vf", 1), "Vf_ones"])
            dump("Csb", Csb, ["Csb"])
            dump("Rsb", Rsb, ["Rsb"])
        if stage("B"):
            break

        et_rr = {"i": 0}
        for u in range(8):
            ub = u % 2
            slot_w, tok_w = wget((s, "u", u))
            wq = wview(slot_w, 0, 8, 128)
            wk = wview(slot_w, 1024, 8, 128)
            for tg in range(4):
                bk = nbank(B03)
                for kc in range(8):
                    I("pe", "matmul", hT_toks(tg) + [tok_w], [("ps", bk)], ps(bk), lhsT=wq[:, kc, :], rhs=hT(tg)[:, kc, :],
                      start=(kc == 0), stop=(kc == 7))
                I("act", "activation", [("ps", bk), "qT0"], [("qT", tg, 0)], out=qT[0:64, 0, tg * 512:(tg + 1) * 512], in_=ps(bk)[0:64, :],
                  func=AF.Copy, scale=0.125)
                I("act", "activation", [("ps", bk), "qT0"], [("qT", tg, 1)], out=qT[64:128, 1, tg * 512:(tg + 1) * 512], in_=ps(bk)[64:128, :],
                  func=AF.Copy, scale=0.125)
                bk = nbank(B03)
                for kc in range(8):
                    I("pe", "matmul", hT_toks(tg) + [tok_w], [("ps", bk)], ps(bk), lhsT=wk[:, kc, :], rhs=hT(tg)[:, kc, :],
                      start=(kc == 0), stop=(kc == 7))
                I("dve", "tensor_copy", [("ps", bk), "kTz0"], [("kT", tg, 0)], out=kTz[0:64, 0, tg * 512:(tg + 1) * 512], in_=ps(bk)[0:64, :])
                I("dve", "tensor_copy", [("ps", bk), "kTz0"], [("kT", tg, 1)], out=kTz[64:128, 1, tg * 512:(tg + 1) * 512], in_=ps(bk)[64:128, :])
            if u == 4:
                I("pool", "memset", [], ["kTz0"], kTz[64:65, 0, :], 1.0)
                I("pool", "memset", [], ["kTz0"], kTz[0:1, 1, :], 1.0)
            if u >= 4:
                for m in range(2):
                    hh = 2 * (u - 4) + m
                    r0 = 64 if m == 0 else 0
                    I("dve", "tensor_scalar", ["Rsb", "qT0"], ["qT0"], out=qT[r0:r0 + 1, m, :].rearrange("p (j n) -> p j n", n=128),
                      in0=Rsb[r0:r0 + 1, hh, :].unsqueeze(2).to_broadcast([1, 16, 128]), scalar1=-1.0, scalar2=None, op0=ALU.mult)
            wdone((s, "u", u))
            if debug and s == 0 and u in (0, 4):
                dump("qT_u%d" % u, qT[:, :, 0:512], [("qT", 0, 0), ("qT", 0, 1), "qT0"])
                dump("kT_u%d" % u, kTz[:, :, 0:256], [("kT", 0, 0), ("kT", 0, 1), "kTz0"])
            for g in range(4):
                attention_group(s, u, ub, g, et_rr)
            if debug and s == 0 and u in (0, 4):
                fin_step(flush=True)
                dump("oT_u%d" % u, oT(0)[:, u, :], [("oT", 0, u, qq) for qq in range(4)])
            if stage("C%d" % u):
                break
        if done["stop"]:
            break
        fin_step(flush=True)

        wget((s, 0, "g0"))
        P.barrier()
        for tg in range(4):
            wg = {}
            wg[0] = wget((s, tg, "g0"))
            wg[2] = wget((s, tg, "g2"))
            s_pa, t_pa = wget((s, tg, "pl"))
            t_pb = t_pa
            wpa = wview(s_pa, 0, 4, 512)
            wpb = wview(s_pa, 2048, 4, 512)
            for n_ in range(8):
                if n_ == 4:
                    wdone((s, tg, "g0"))
                    wdone((s, tg, "g2"))
                    wdone((s, tg, "pl"))
                    wg[1] = wget((s, tg, "g1"))
                    wg[3] = wget((s, tg, "g3"))
                    s_pa, t_pa = wget((s, tg, "ph"))
                    t_pb = t_pa
                    wpa = wview(s_pa, 0, 4, 512)
                    wpb = wview(s_pa, 2048, 4, 512)
                s0, t0_ = wg[n_ // 4]
                s1, t1_ = wg[2 + n_ // 4]
                co = (n_ % 4) * 128
                w0 = wview(s0, 0, 8, 512)
                w1v = wview(s1, 0, 8, 512)
                pb2 = n_ % 2
                bG0, bG1, bA, bB = nbank(B07), nbank(B07), nbank(B07), nbank(B07)
                for kc in range(8):
                    I("pe", "matmul", hT_toks(tg) + [t0_], [("ps", bG0)], ps(bG0), lhsT=w0[:, kc, co:co + 128], rhs=hT(tg)[:, kc, :],
                      start=(kc == 0), stop=(kc == 7))
                for kc in range(8):
                    I("pe", "matmul", hT_toks(tg) + [t1_], [("ps", bG1)], ps(bG1), lhsT=w1v[:, kc, co:co + 128], rhs=hT(tg)[:, kc, :],
                      start=(kc == 0), stop=(kc == 7))
                for fc in range(4):
                    I("pe", "matmul", oT_toks(tg) + [t_pa], [("ps", bA)], ps(bA), lhsT=wpa[:, fc, co:co + 128], rhs=oT(tg)[:, fc, :],
                      start=(fc == 0), stop=(fc == 3))
                for fc in range(4):
                    I("pe", "matmul", oT_toks(tg) + [t_pb], [("ps", bB)], ps(bB), lhsT=wpb[:, fc, co:co + 128], rhs=oT(tg)[:, 4 + fc, :],
                      start=(fc == 0), stop=(fc == 3))
                I("act", "activation", [("ps", bG0)], [("sg", pb2, 0)], out=sg[:, pb2, 0, :], in_=ps(bG0), func=AF.Sigmoid)
                I("act", "activation", [("ps", bG1)], [("sg", pb2, 1)], out=sg[:, pb2, 1, :], in_=ps(bG1), func=AF.Sigmoid)
                I("dve", "tensor_tensor", [("sg", pb2, 0), ("ps", bA)], [("m01", pb2, 0)], out=m01[:, pb2, 0, :], in0=sg[:, pb2, 0, :], in1=ps(bA), op=ALU.mult)
                I("dve", "tensor_tensor", [("sg", pb2, 1), ("ps", bB)], [("m01", pb2, 1)], out=m01[:, pb2, 1, :], in0=sg[:, pb2, 1, :], in1=ps(bB), op=ALU.mult)
                I("pool", "tensor_tensor", [("m01", pb2, 0), ("m01", pb2, 1)], [("mT", n_)], out=mT[:, n_, :], in0=m01[:, pb2, 0, :], in1=m01[:, pb2, 1, :],
                  op=ALU.add)
            wdone((s, tg, "g1"))
            wdone((s, tg, "g3"))
            wdone((s, tg, "ph"))
            wo_s = [wget((s, tg, "wo0")), wget((s, tg, "wo1"))]
            for tt in range(4):
                i = tg * 4 + tt
                slot = (16 + i) % 3
                I("sp", "dma_start", [], [("xr", slot)], out=xring[:, slot, :], in_=x_d[row0 + i * 128:row0 + (i + 1) * 128, :], dma=True)
                for mh in range(2):
                    so, to = wo_s[mh]
                    wov = wview(so, 0, 8, 512)
                    bk = nbank(B07)
                    for n_ in range(8):
                        I("pe", "matmul", [("mT", n_), to], [("ps", bk)], ps(bk), lhsT=mT[:, n_, tt * 128:(tt + 1) * 128], rhs=wov[:, n_, :],
                          start=(n_ == 0), stop=(n_ == 7))
                    I("dve", "tensor_tensor", [("ps", bk), ("xr", slot)], hT_toks(tg) + oT_toks(tg) + [("x1", tg, tt, mh)],
                      out=x1(tg)[:, tt, mh * 512:(mh + 1) * 512], in0=ps(bk), in1=xring[:, slot, mh * 512:(mh + 1) * 512], op=ALU.add)
            wdone((s, tg, "wo0"))
            wdone((s, tg, "wo1"))
            for tt in range(4):
                b2 = tt % 2
                ssa = sm2[:, b2:b2 + 1]
                rsa = sm2[:, 4 + b2:5 + b2]
                tma = sm2[:, 8 + b2:9 + b2]
                x1t = [("x1", tg, tt, 0), ("x1", tg, tt, 1)]
                I("act", "activation", x1t, ["junk2", ("ssF", b2)], out=junk2, in_=x1(tg)[:, tt, :], func=AF.Square, accum_out=ssa)
                rstd_chain(ssa, rsa, 1.0 / D, [("ssF", b2)], ("rsF", b2), tma)
                I("pool", "tensor_scalar", x1t + [("rsF", b2)], [("xn2", b2)], out=xn2[:, b2, :], in0=x1(tg)[:, tt, :], scalar1=rsa, scalar2=1.0,
                  op0=ALU.mult, op1=ALU.mult)
                bk = nbank(B07)
                for c in range(8):
                    I("pe", "transpose", [("xn2", b2), "ident_bf"], [("ps", bk)], out=ps_bf(bk)[:, c * 128:(c + 1) * 128],
                      in_=xn2[:, b2, c * 128:(c + 1) * 128], identity=ident_bf)
                I("dve", "tensor_tensor", [("ps", bk), "gmlp_c"], [("h2T", tg, tt)], out=h2T[:, :, tg * 512 + tt * 128:tg * 512 + (tt + 1) * 128],
                  in0=ps_bf(bk).rearrange("p (c n) -> p c n", c=8), in1=gmlp_c.unsqueeze(2).to_broadcast([128, 8, 128]), op=ALU.mult)
        if debug and s == 0:
            dump("x1_0", x1(0)[:, 0, :], [("x1", 0, 0, 0), ("x1", 0, 0, 1)])
            dump("h2T", h2T[:, :, 0:128], [("h2T", 0, 0)])
        if stage("E"):
            break

        for e8 in range(8):
            s_w1, t_w1 = wget((s, "w1", e8))
            s_w2, t_w2 = wget((s, "w2", e8))
            w1v = wview(s_w1, 0, 8, 512)
            w2v = wview(s_w2, 0, 4, 1024)
            for tg in range(4):
                ab = (e8 * 4 + tg) % 2
                h2t = [("h2T", tg, tt) for tt in range(4)]
                for fl in range(4):
                    bk = nbank(B03)
                    rb = fl % 2
                    for kc in range(8):
                        I("pe", "matmul", h2t + [t_w1], [("ps", bk)], ps(bk), lhsT=w1v[:, kc, fl * 128:(fl + 1) * 128],
                          rhs=h2T[:, kc, tg * 512:(tg + 1) * 512], start=(kc == 0), stop=(kc == 7))
                    I("act", "activation", [("ps", bk)], [("rr", rb)], out=rr[:, rb, :], in_=ps(bk), func=AF.Relu)
                    I("pool", "tensor_tensor", [("rr", rb)], [("aT", ab, fl)], out=aT[:, ab, fl, :], in0=rr[:, rb, :], in1=rr[:, rb, :], op=ALU.mult)
                for tt in range(4):
                    for mh in range(2):
                        bk = nbank(B47)
                        for fl in range(4):
                            I("pe", "matmul", [("aT", ab, fl), t_w2], [("ps", bk)], ps(bk), lhsT=aT[:, ab, fl, tt * 128:(tt + 1) * 128],
                              rhs=w2v[:, fl, mh * 512:(mh + 1) * 512], start=(fl == 0), stop=(fl == 3))
                        I("dve", "tensor_tensor", [("ps", bk), ("x1", tg, tt, mh)], [("x1", tg, tt, mh)], out=x1(tg)[:, tt, mh * 512:(mh + 1) * 512],
                          in0=ps(bk), in1=x1(tg)[:, tt, mh * 512:(mh + 1) * 512], op=ALU.add)
            wdone((s, "w1", e8))
            wdone((s, "w2", e8))
        if debug and s == 0:
            dump("x2_0", x1(0)[:, 0, :], [("x1", 0, 0, 0), ("x1", 0, 0, 1)])

        if s + 1 < nseq:
            wget((s + 1, "dv"))
        for i in range(16):
            tg, tt = i // 4, i % 4
            b2 = i % 2
            slot = (32 + i) % 3
            ssa = sm2[:, 16 + b2:17 + b2]
            rsa = sm2[:, 20 + b2:21 + b2]
            tma = sm2[:, 24 + b2:25 + b2]
            x1t = [("x1", tg, tt, 0), ("x1", tg, tt, 1)]
            I("act", "activation", x1t, ["junk2", ("ssG", b2)], out=junk2, in_=x1(tg)[:, tt, :], func=AF.Square, accum_out=ssa)
            rstd_chain(ssa, rsa, 1.0 / D, [("ssG", b2)], ("rsG", b2), tma)
            I("dve", "scalar_tensor_tensor", x1t + [("rsG", b2), "gfin_b"], [("xr", slot)], out=xring[:, slot, :], in0=x1(tg)[:, tt, :], scalar=rsa,
              in1=gfin_b, op0=ALU.mult, op1=ALU.mult)
            I("sp", "dma_start", [("xr", slot)], [("out", s, i)], out=out_d[row0 + i * 128:row0 + (i + 1) * 128, :], in_=xring[:, slot, :], dma=True)

    es = ExitStack()
    with es:
        es.enter_context(nc.allow_non_contiguous_dma(reason="tiny strided parameter loads"))
        P.emit(nc, es)
    return nc, dbg_map


def _rel_bucket_np(rel):
    nb = 16
    ret = np.where(rel > 0, nb, 0)
    n = np.abs(rel)
    max_exact = nb // 2
    nf = np.maximum(n, 1).astype(np.float32)
    large = max_exact + (np.log(nf / np.float32(max_exact)) / np.float32(math.log(128 / max_exact))
                         * np.float32(nb - max_exact)).astype(np.int32)
    large = np.minimum(large, nb - 1)
    return ret + np.where(n < max_exact, n, large)


def _constants():
    k = np.arange(128)[:, None]
    q = np.arange(128)[None, :]
    cst = np.zeros((128, 6, 128), np.float32)
    cst[:, 0, :] = np.eye(128, dtype=np.float32)
    cst[:, 1, :] = (k <= q).astype(np.float32)
    cst[:, 2, :] = 1.0
    cst[:, 3, :] = np.where(k <= q, 0.0, NEG)
    seg = np.ones((8, 16), np.float32)
    seg[:, 0] = 0.0
    cst[:, 4, :] = seg.reshape(1, 128)
    cst[:, 5, :] = np.where((k // 64) <= (q // 64), 0.0, NEG)
    idx = np.zeros((128, 2, 128), np.int64)
    idx[:, 0, :] = _rel_bucket_np((k - q).astype(np.int32))
    idx[:, 1, :] = _rel_bucket_np((k - q - 128).astype(np.int32))
    return cst.reshape(128, 6 * 128), idx


_CACHE = {}


def kernel(**inputs):
    x = np.ascontiguousarray(np.asarray(inputs["x"], np.float32))
    cst, idx = _constants()
    rel = np.asarray(inputs["rel_table"], np.float32)
    braw = np.ascontiguousarray(np.transpose(rel[idx], (0, 3, 1, 2))).reshape(128, 4 * 2 * 128)
    lamv = np.concatenate([np.asarray(inputs[k], np.float32).reshape(-1) for k in ("lam_q1", "lam_k1", "lam_q2", "lam_k2")])
    shared = {
        "w_in": np.ascontiguousarray(np.asarray(inputs["w_in"], np.float32)[0]),
        "w_pa": np.ascontiguousarray(np.asarray(inputs["w_pa"], np.float32)[0]),
        "w_pb": np.ascontiguousarray(np.asarray(inputs["w_pb"], np.float32)[0]),
        "w_o": np.ascontiguousarray(np.asarray(inputs["w_o"], np.float32)[0]),
        "w_1": np.ascontiguousarray(np.asarray(inputs["w_1"], np.float32)[0]),
        "w_2": np.ascontiguousarray(np.asarray(inputs["w_2"], np.float32)[0]),
        "g_mix": np.asarray(inputs["g_mix"], np.float32).reshape(-1),
        "g_mlp": np.asarray(inputs["g_mlp"], np.float32).reshape(-1),
        "g_final": np.asarray(inputs["g_final"], np.float32).reshape(-1),
        "g_subln": np.asarray(inputs["g_subln"], np.float32).reshape(-1),
        "b_f": np.asarray(inputs["b_f"], np.float32).reshape(-1),
        "lamv": lamv,
        "rel_table": rel,
        "cst": cst,
        "bias_raw": braw,
    }
    if "nc" not in _CACHE:
        _CACHE["nc"] = build()[0]
    nc = _CACHE["nc"]
    in_maps = []
    for c in range(NCORES):
        m = dict(shared)
        m["x"] = x[NSEQ * c:NSEQ * (c + 1)].reshape(NSEQ * SEQ, D)
        in_maps.append(m)
    res = run_bass_kernel_spmd(nc, in_maps, core_ids=list(range(NCORES)))
    out = np.concatenate([np.asarray(r["out"]).reshape(NSEQ, SEQ, D) for r in res.results], axis=0)
    return out.astype(np.float32)
```
